# Optimizing a Trainium2 kernel written in Bass

```python
import math
import jax
import jax.numpy as jnp
from jax import lax
import numpy as np

D_MODEL = 1024
BATCH = 4
SEQ = 8192
DEPTH = 2

N_MEM = 256
GROUP_WIDTH = 512
D_MIX = 4 * GROUP_WIDTH
NORM_EPS = 1e-6

RET_HEADS = 4
RET_HEAD_DIM = 128
RET_CHUNK = 128
ROPE_BASE = 10000.0

SSD_HEADS = 8
SSD_HEAD_DIM = 64
SSD_GROUPS = 2
SSD_STATE = 128
SSD_CONV = 4
SSD_CHUNK = 128
SSD_BC = SSD_GROUPS * SSD_STATE
SSD_CONV_DIM = GROUP_WIDTH + 2 * SSD_BC
SSD_NORM_GROUP = GROUP_WIDTH // SSD_GROUPS
DT_MIN = 1e-3
DT_MAX = 1e-1
A_INIT_MIN = 1.0
A_INIT_MAX = 16.0

SB_HEADS = 4
SB_HEAD_DIM = 128
SB_BLOCK = 128

MEM_HEADS = 4
MEM_HEAD_DIM = 128

IN_SIZES = (GROUP_WIDTH, GROUP_WIDTH, GROUP_WIDTH, GROUP_WIDTH,
            SSD_CONV_DIM, GROUP_WIDTH, SSD_HEADS,
            GROUP_WIDTH, GROUP_WIDTH, GROUP_WIDTH, GROUP_WIDTH,
            GROUP_WIDTH, GROUP_WIDTH)
D_IN = sum(IN_SIZES)

kernel_name = 'hymba_style_ret_ssd_stickbreak_mem'


def rmsnorm(x, g):
    xf = x.astype(jnp.float32)
    y = xf * lax.rsqrt(jnp.mean(xf * xf, axis=-1, keepdims=True) + NORM_EPS)
    return (y * g.astype(jnp.float32)).astype(x.dtype)


def apply_rope(t):
    s, d = t.shape[1], t.shape[-1]
    inv_freq = ROPE_BASE ** (-jnp.arange(0, d, 2, dtype=jnp.float32) / d)
    ang = jnp.arange(s, dtype=jnp.float32)[:, None] * inv_freq[None, :]
    cos = jnp.cos(ang)[None, :, None, :]
    sin = jnp.sin(ang)[None, :, None, :]
    t1, t2 = jnp.split(t.astype(jnp.float32), 2, axis=-1)
    return jnp.concatenate([t1 * cos - t2 * sin, t2 * cos + t1 * sin], axis=-1).astype(t.dtype)


def retention(q, k, v):
    b, s, h, d = q.shape
    dv = v.shape[-1]
    c = RET_CHUNK
    nc = s // c
    dt = v.dtype
    q = apply_rope(q)
    k = apply_rope(k) * (d ** -0.5)
    log_g = jnp.log(1.0 - 2.0 ** (-5.0 - jnp.arange(h, dtype=jnp.float32)))
    idx = jnp.arange(c, dtype=jnp.float32)
    rel = idx[:, None] - idx[None, :]
    intra = jnp.where(rel[None] >= 0, jnp.exp(rel[None] * log_g[:, None, None]), 0.0)
    zeta = jnp.exp((c - 1 - idx)[:, None] * log_g[None, :])
    xi = jnp.exp((idx + 1)[:, None] * log_g[None, :])
    chunk_decay = jnp.exp(c * log_g)
    qc = q.reshape(b, nc, c, h, d)
    kc = k.reshape(b, nc, c, h, d)
    vc = v.reshape(b, nc, c, h, dv)
    scores = jnp.einsum('bcnhd,bcmhd->bchnm', qc, kc) * intra.astype(dt)
    y_intra = jnp.einsum('bchnm,bcmhe->bcnhe', scores, vc)
    kv = jnp.einsum('bcmhd,bcmhe->bchde', kc * zeta.astype(dt)[:, :, None], vc)

    def step(state, kv_c):
        return chunk_decay[:, None, None] * state + kv_c, state

    _, r_prev = lax.scan(step, jnp.zeros((b, h, d, dv), jnp.float32),
                         jnp.moveaxis(kv.astype(jnp.float32), 1, 0))
    r_prev = jnp.moveaxis(r_prev, 0, 1).astype(dt)
    y_inter = jnp.einsum('bcnhd,bchde->bcnhe', qc, r_prev) * xi.astype(dt)[:, :, None]
    y = (y_intra + y_inter).reshape(b, s, h, dv).astype(jnp.float32)
    mu = jnp.mean(y, axis=-1, keepdims=True)
    var = jnp.mean(jnp.square(y - mu), axis=-1, keepdims=True)
    y = (y - mu) * lax.rsqrt(var + NORM_EPS)
    return y.reshape(b, s, h * dv).astype(dt)


def ssd_mixer(xbc, z, dt_raw, conv_w, conv_b, dt_bias, a_log, d_skip, norm_g):
    b, s, _ = xbc.shape
    dtype = xbc.dtype
    L = SSD_CHUNK
    nc = s // L
    G, J, P, N = SSD_GROUPS, SSD_HEADS // SSD_GROUPS, SSD_HEAD_DIM, SSD_STATE
    xbc = lax.conv_general_dilated(xbc, conv_w[:, None, :], window_strides=(1,),
                                   padding=[(SSD_CONV - 1, 0)],
                                   dimension_numbers=('NWC', 'WIO', 'NWC'),
                                   feature_group_count=SSD_CONV_DIM)
    xbc = jax.nn.silu(xbc + conv_b)
    xs, bm, cm = jnp.split(xbc, [GROUP_WIDTH, GROUP_WIDTH + SSD_BC], axis=-1)
    delta = jax.nn.softplus(dt_raw.astype(jnp.float32) + dt_bias.astype(jnp.float32))
    da = (delta * -jnp.exp(a_log.astype(jnp.float32))).reshape(b, nc, L, SSD_HEADS)
    x_heads = xs.reshape(b, s, SSD_HEADS, P)
    xdt = (x_heads * delta[..., None].astype(dtype)).reshape(b, nc, L, G, J, P)
    bc = bm.reshape(b, nc, L, G, N)
    cc = cm.reshape(b, nc, L, G, N)
    a_cs = jnp.cumsum(da, axis=2)
    causal = jnp.tril(jnp.ones((L, L), dtype=bool))[None, None, :, :, None]
    seg = a_cs[:, :, :, None, :] - a_cs[:, :, None, :, :]
    decay_in = jnp.exp(jnp.where(causal, seg, -jnp.inf)).reshape(b, nc, L, L, G, J)
    cb = jnp.einsum('bclgn,bcsgn->bclsg', cc, bc)
    y_diag = jnp.einsum('bclsgj,bcsgjp->bclgjp', cb[..., None] * decay_in.astype(dtype), xdt)
    decay_states = jnp.exp(a_cs[:, :, -1:, :] - a_cs).reshape(b, nc, L, G, J)
    states = jnp.einsum('bclgn,bclgjp->bcgjpn', bc, xdt * decay_states.astype(dtype)[..., None])
    chunk_decay = jnp.exp(a_cs[:, :, -1, :]).reshape(b, nc, G, J)

    def step(state, inp):
        dec, st = inp
        return dec[..., None, None] * state + st, state

    _, prev = lax.scan(step, jnp.zeros((b, G, J, P, N), jnp.float32),
                       (jnp.moveaxis(chunk_decay, 1, 0), jnp.moveaxis(states.astype(jnp.float32), 1, 0)))
    prev = jnp.moveaxis(prev, 0, 1).astype(dtype)
    y_off = jnp.einsum('bclgn,bcgjpn->bclgjp', cc, prev) * jnp.exp(a_cs).reshape(b, nc, L, G, J, 1).astype(dtype)
    y = (y_diag + y_off).reshape(b, s, SSD_HEADS, P) + x_heads * d_skip.astype(dtype)[:, None]
    y = y.reshape(b, s, GROUP_WIDTH) * jax.nn.silu(z)
    y = rmsnorm(y.reshape(b, s, SSD_GROUPS, SSD_NORM_GROUP), norm_g.reshape(SSD_GROUPS, SSD_NORM_GROUP))
    return y.reshape(b, s, GROUP_WIDTH)


def stick_breaking(q, k, v):
    b, s, h, d = q.shape
    nb = s // SB_BLOCK
    scale = d ** -0.5
    key_pos = jnp.arange(s)
    qb = jnp.moveaxis(q.reshape(b, nb, SB_BLOCK, h, d), 1, 0)

    def block(args):
        q_blk, i = args
        logits = jnp.einsum('bthd,bshd->bhts', q_blk, k).astype(jnp.float32) * scale
        q_pos = i * SB_BLOCK + jnp.arange(SB_BLOCK)
        causal = key_pos[None, :] < q_pos[:, None]
        log_beta = jax.nn.log_sigmoid(logits)
        log_1mb = jnp.where(causal, jax.nn.log_sigmoid(-logits), 0.0)
        tail = lax.cumsum(log_1mb, axis=3, reverse=True) - log_1mb
        w = jnp.where(causal, jnp.exp(log_beta + tail), 0.0)
        return jnp.einsum('bhts,bshd->bthd', w.astype(v.dtype), v)

    out = lax.map(block, (qb, jnp.arange(nb)))
    return jnp.moveaxis(out, 0, 1).reshape(b, s, h * d)


def memory_read(q, mem_n, w_mem_kv):
    b, s, h, d = q.shape
    kv = mem_n @ w_mem_kv
    k, v = jnp.split(kv, 2, axis=-1)
    k = k.reshape(b, -1, h, d)
    v = v.reshape(b, -1, h, d)
    logits = jnp.einsum('bshd,bmhd->bhsm', q, k).astype(jnp.float32) * (d ** -0.5)
    p = jax.nn.softmax(logits, axis=-1).astype(v.dtype)
    return jnp.einsum('bhsm,bmhd->bshd', p, v).reshape(b, s, h * d)


def hybrid_layer(x, mem, pre_g, post_g, w_in, conv_w, conv_b, dt_bias, a_log, d_skip,
                 ssd_norm_g, mem_norm_g, w_mem_kv, w_out):
    b, s, _ = x.shape
    hn = rmsnorm(x, pre_g)
    proj = hn @ w_in
    (ret_q, ret_k, ret_v, ret_g, xbc, z, dt_raw,
     sb_q, sb_k, sb_v, sb_g, mem_q, mem_g) = jnp.split(proj, list(np.cumsum(IN_SIZES)[:-1]), axis=-1)
    y_ret = retention(ret_q.reshape(b, s, RET_HEADS, RET_HEAD_DIM),
                      ret_k.reshape(b, s, RET_HEADS, RET_HEAD_DIM),
                      ret_v.reshape(b, s, RET_HEADS, RET_HEAD_DIM)) * jax.nn.silu(ret_g)
    y_ssd = ssd_mixer(xbc, z, dt_raw, conv_w, conv_b, dt_bias, a_log, d_skip, ssd_norm_g)
    y_sb = stick_breaking(sb_q.reshape(b, s, SB_HEADS, SB_HEAD_DIM),
                          sb_k.reshape(b, s, SB_HEADS, SB_HEAD_DIM),
                          sb_v.reshape(b, s, SB_HEADS, SB_HEAD_DIM)) * jax.nn.silu(sb_g)
    y_mem = memory_read(mem_q.reshape(b, s, MEM_HEADS, MEM_HEAD_DIM),
                        rmsnorm(mem, mem_norm_g), w_mem_kv) * jax.nn.silu(mem_g)
    y = jnp.concatenate([y_ret, y_ssd, y_sb, y_mem], axis=-1) @ w_out
    return x + rmsnorm(y, post_g)


def setup_inputs(seed: int = 0) -> dict:
    key = jax.random.key(seed)
    ks = jax.random.split(key, 14)
    f32 = jnp.float32

    def gain(k, shape):
        return 1.0 + 0.02 * jax.random.normal(k, shape, f32)

    dt0 = jnp.exp(jax.random.uniform(ks[7], (DEPTH, SSD_HEADS), f32, math.log(DT_MIN), math.log(DT_MAX)))
    return {
        'x': jax.random.normal(ks[0], (BATCH, SEQ, D_MODEL), f32),
        'mem': jax.random.normal(ks[1], (BATCH, N_MEM, D_MODEL), f32),
        'pre_norm_g': gain(ks[2], (DEPTH, D_MODEL)),
        'post_norm_g': gain(ks[3], (DEPTH, D_MODEL)),
        'w_in': jax.random.normal(ks[4], (DEPTH, D_MODEL, D_IN), f32) * D_MODEL ** -0.5,
        'conv_w': jax.random.normal(ks[5], (DEPTH, SSD_CONV, SSD_CONV_DIM), f32) * SSD_CONV ** -0.5,
        'conv_b': 0.02 * jax.random.normal(ks[6], (DEPTH, SSD_CONV_DIM), f32),
        'dt_bias': dt0 + jnp.log(-jnp.expm1(-dt0)),
        'a_log': jnp.log(jax.random.uniform(ks[8], (DEPTH, SSD_HEADS), f32, A_INIT_MIN, A_INIT_MAX)),
        'd_skip': gain(ks[9], (DEPTH, SSD_HEADS)),
        'ssd_norm_g': gain(ks[10], (DEPTH, GROUP_WIDTH)),
        'mem_norm_g': gain(ks[11], (DEPTH, D_MODEL)),
        'w_mem_kv': jax.random.normal(ks[12], (DEPTH, D_MODEL, 2 * GROUP_WIDTH), f32) * D_MODEL ** -0.5,
        'w_out': jax.random.normal(ks[13], (DEPTH, D_MIX, D_MODEL), f32) * D_MIX ** -0.5,
    }


def reference(x, mem, pre_norm_g, post_norm_g, w_in, conv_w, conv_b, dt_bias, a_log, d_skip,
              ssd_norm_g, mem_norm_g, w_mem_kv, w_out):
    for l in range(DEPTH):
        x = hybrid_layer(x, mem, pre_norm_g[l], post_norm_g[l], w_in[l], conv_w[l], conv_b[l],
                         dt_bias[l], a_log[l], d_skip[l], ssd_norm_g[l], mem_norm_g[l],
                         w_mem_kv[l], w_out[l])
    return x
```

```python
import math
from contextlib import ExitStack
import numpy as np
import concourse.bass as bass
import concourse.mybir as mybir
from concourse.bass_utils import run_bass_kernel_spmd

F32 = mybir.dt.float32
BF16 = mybir.dt.bfloat16
AF = mybir.ActivationFunctionType
ALU = mybir.AluOpType
AX = mybir.AxisListType

SAME_ENGINE_SYNC = True
RAW_ONLY_SAME_ENGINE = False
N_DMA_SEMS = 6
DMA_QUEUES = ("sp", "act")
D = 1024
NFM = 1792
NTM = 1540
EPS = 1e-6


class _Op:
    __slots__ = ("eng", "fn", "deps", "is_dma", "signal", "tok", "idx")


class _Rec:
    def __init__(self):
        self.call = None

    def __getattr__(self, name):
        def f(*a, **k):
            self.call = (name, a, k)
            return None
        return f


class Prog:
    ENGS = ("pe", "act", "dve", "pool", "sp")

    def __init__(self, nc):
        self.nc = nc
        self.ops = []
        self.last_w = {}
        self.readers = {}

    ALIAS = {"H2a": ["H2.0", "H2.1"], "H2b": ["H2.2", "H2.3"], "H3c": ["H3.2", "H3.3"], "F2c": ["F2.2"],
             "H6a": ["H6.0", "H6.1"], "H6b": ["H6.2", "H6.3"], "F1z": ["F1.2", "F1.3"]}

    @classmethod
    def xk(cls, keys):
        out = []
        for k in keys:
            if k in cls.ALIAS:
                out += cls.ALIAS[k]
            elif len(k) == 2 and k[0] in "FH" and k[1].isdigit():
                out += [f"{k}.{q}" for q in range(4)]
            else:
                out.append(k)
        return out

    def capture_begin(self):
        self._cap = []

    def capture_end(self):
        c = self._cap
        self._cap = None
        return c

    def mark(self):
        return len(self._cap)

    @staticmethod
    def interleave(a, b):
        out = []
        na, nb = len(a), len(b)
        ia = ib = 0
        while ia < na or ib < nb:
            if ib >= nb or (ia < na and ia * nb <= ib * na):
                out.append(a[ia]); ia += 1
            else:
                out.append(b[ib]); ib += 1
        return out

    def register(self, ops):
        for o in ops:
            self._reg(*o)

    def merge(self, a, b):
        na, nb = len(a), len(b)
        ia = ib = 0
        while ia < na or ib < nb:
            if ib >= nb or (ia < na and ia * nb <= ib * na):
                self._reg(*a[ia])
                ia += 1
            else:
                self._reg(*b[ib])
                ib += 1

    def op(self, eng, fn, reads=(), writes=(), dma=False):
        rec = _Rec()
        fn(rec)
        if getattr(self, "_cap", None) is not None:
            self._cap.append((eng, rec.call, reads, writes, dma))
            return None
        return self._reg(eng, rec.call, reads, writes, dma)

    def _reg(self, eng, call, reads=(), writes=(), dma=False):
        reads = self.xk(reads)
        writes = self.xk(writes)
        pr = [k for k in reads if k == "tb" or (k[0] == "p" and k[1:].isdigit())]
        if pr:
            reads = [k for k in reads if k not in pr]
            writes = list(writes) + pr
        o = _Op()
        o.eng, o.fn, o.is_dma, o.signal, o.tok = eng, call, dma, False, None
        o.idx = len(self.ops)
        deps = set()
        raw = set()
        for k in reads:
            w = self.last_w.get(k)
            if w is not None:
                deps.add(w)
                raw.add(w)
        for k in writes:
            w = self.last_w.get(k)
            if w is not None:
                deps.add(w)
            for r in self.readers.get(k, ()):
                deps.add(r)
        if RAW_ONLY_SAME_ENGINE and not dma:
            deps = {d for d in deps if d in raw or self.ops[d].eng != eng or self.ops[d].is_dma}
        for k in reads:
            self.readers.setdefault(k, []).append(o.idx)
        for k in writes:
            self.last_w[k] = o.idx
            self.readers[k] = []
        deps.discard(o.idx)
        o.deps = deps
        self.ops.append(o)
        return o.idx

    def emit(self, sems, dma_sems):
        ops = self.ops
        for o in ops:
            for d in o.deps:
                p = ops[d]
                if p.is_dma:
                    continue
                if p.eng == o.eng and (not o.is_dma) and (p.eng == "pe" or not SAME_ENGINE_SYNC):
                    continue
                p.signal = True
        cnt = {e: 0 for e in self.ENGS}
        dcnt = {e: [0] * N_DMA_SEMS for e in self.ENGS}
        drr = {e: 0 for e in self.ENGS}
        dma_prev = {}
        ccn = [0]
        per_eng = {e: [] for e in self.ENGS}
        for o in ops:
            pre = None
            if o.is_dma == "cc":
                ccn[0] += 1
                o.tok = (dma_sems["cc"], ccn[0], None)
            elif o.is_dma:
                s = drr[o.eng] % N_DMA_SEMS
                drr[o.eng] += 1
                pre = dma_prev.get((o.eng, s))
                dcnt[o.eng][s] += 16
                o.tok = (dma_sems[o.eng][s], dcnt[o.eng][s], 16)
                dma_prev[(o.eng, s)] = o.tok
            elif o.signal:
                cnt[o.eng] += 1
                o.tok = (sems[o.eng], cnt[o.eng], 1)
            per_eng[o.eng].append((o, pre))
        waited = {e: {} for e in self.ENGS}
        nwaits = 0
        plans = {e: [] for e in self.ENGS}
        for e in self.ENGS:
            for o, pre in per_eng[e]:
                toks = []
                if pre is not None:
                    toks.append(pre)
                for d in o.deps:
                    p = ops[d]
                    if p.tok is None:
                        continue
                    if (not p.is_dma) and (not o.is_dma) and p.eng == e and (e == "pe" or not SAME_ENGINE_SYNC):
                        continue
                    toks.append(p.tok)
                best = {}
                for (sem, val, _) in toks:
                    k = id(sem)
                    if val > waited[e].get(k, 0):
                        if k not in best or best[k][1] < val:
                            best[k] = (sem, val)
                ws = []
                for k, (sem, val) in best.items():
                    waited[e][k] = val
                    ws.append((sem, val))
                    nwaits += 1
                plans[e].append((o, ws))
        self.stats = dict(nops=len(ops), nwaits=nwaits, per_eng={e: len(per_eng[e]) for e in self.ENGS})
        nc = self.nc
        with nc.Block() as block:
            def run(engname, eng):
                for o, ws in plans[engname]:
                    for sem, val in ws:
                        eng.wait_ge(sem, val)
                    inst = None
                    if o.fn is not None:
                        inst = getattr(eng, o.fn[0])(*o.fn[1], **o.fn[2])
                    if inst is not None and o.tok is not None:
                        if o.tok[2] is None:
                            inst.then_inc(o.tok[0])
                        else:
                            inst.then_inc(o.tok[0], o.tok[2])

            @block.tensor
            def _(eng):
                run("pe", eng)

            @block.scalar
            def _(eng):
                run("act", eng)

            @block.vector
            def _(eng):
                run("dve", eng)

            @block.gpsimd
            def _(eng):
                run("pool", eng)

            @block.sync
            def _(eng):
                run("sp", eng)


def build_fused(S, depth=2, rgroups=None):
    NG = S // 512
    NCH = S // 128
    nc = bass.Bass("TRN2", target_bir_lowering=False)

    def din(name, shape):
        return nc.dram_tensor(name, shape, F32, kind="ExternalInput").ap()

    x_d = din("x", [S, D])
    mem_d = din("memx", [256, D])
    if rgroups is None:
        rgroups = [[0, 1], [2, 3], [4, 5], [6, 7]]
    wfm_ds = [din(f"wfm{l}", [D, NFM]) for l in range(depth)]
    wtm_ds = [din(f"wtm{l}", [D, NTM]) for l in range(depth)]
    wout_ds = [din(f"wout{l}", [D, D]) for l in range(depth)]
    wmem_ds = [din(f"wmem{l}", [D, 512]) for l in range(depth)]
    gains_ds = [din(f"gains{l}", [128, 16]) for l in range(depth)]
    convp_ds = [din(f"convp{l}", [128, 20]) for l in range(depth)]
    rowv_ds = [din(f"rowv{l}", [1, 528]) for l in range(depth)]
    postg_ds = [din(f"postg{l}", [1, D]) for l in range(depth)]
    Pd = [nc.dram_tensor(f"Pd{l}", [S, D], F32, kind="Internal", addr_space="Local").ap() for l in range(depth)]
    Yd = [nc.dram_tensor(f"Yd{l}", [S, D], F32, kind="Internal", addr_space="Local").ap() for l in range(depth)]
    Xd = [x_d] + [nc.dram_tensor(f"Xd{l}", [S, D], F32, kind="Internal", addr_space="Local").ap() for l in range(1, depth)]
    cf_d = din("cf", [128, 640 + 896 + 128])
    rtab_d = din("rtab", [NCH, 128, 1024])
    out_d = nc.dram_tensor("out", [S, D], F32, kind="ExternalOutput").ap()

    es = ExitStack()
    with es:
        def sb(name, shape, dt=F32):
            return es.enter_context(nc.sbuf_tensor(name, shape, dt))

        def psum(name, shape, dt=F32):
            return es.enter_context(nc.psum_tensor(name, shape, dt))

        Wfm = sb("Wfm", [128, 8, NFM], BF16)
        Wtm = sb("Wtm", [128, 8, NTM], BF16)
        Wout = sb("Wout", [128, 8, D], BF16)
        KT = sb("KT", [128, 2, S], BF16)
        VV = sb("VV", [128, 2, NCH, 128], BF16)
        hnT = sb("hnT", [128, 8, 512], BF16)
        memT = hnT
        yT = sb("yT", [128, 8, 512], BF16)
        Wmem = yT
        qT = sb("qT", [128, 2, 512], BF16)
        gsb = sb("gsb", [128, 2, 512], BF16)
        qmT = sb("qmT", [128, 2, 512], BF16)
        gm = sb("gm", [128, 2, 512], BF16)
        xcT = sb("xcT", [128, 4, 512], BF16)
        Fs = [sb(f"F{i}", [128, 520], F32) for i in range(5)]
        Hs = [sb(f"H{i}", [128, 512], BF16) for i in range(8)]
        xt = sb("xt", [128, D], F32)
        ox = sb("ox", [128, D], F32)
        hn = sb("hn", [128, D], BF16)
        rt = sb("rt", [128, 1024], F32)
        gains = sb("gains_s", [128, 16], F32)
        convp = sb("convp_s", [128, 20], F32)
        rowv = sb("rowv_s", [128, 528], F32)
        LEf = sb("LEf", [128, 128], F32)
        GTf = sb("GTf", [128, 128], F32)
        onesf = sb("onesf", [128, 128], F32)
        identb = sb("identb", [128, 128], BF16)
        GEb = sb("GEb", [128, 128], BF16)
        LTb = sb("LTb", [128, 128], BF16)
        onesb = sb("onesb", [128, 128], BF16)
        negm = sb("negm", [128, 896], BF16)
        kmT = sb("kmT", [128, 2, 256], BF16)
        vm = sb("vm", [128, 2, 256], BF16)
        Rf = sb("Rf", [128, 2, 128], F32)
        Rb = sb("Rb", [128, 2, 128], BF16)
        Sf = sb("Sf", [128, 256], F32)
        Sb = sb("Sb", [128, 256], BF16)
        halo = sb("halo", [128, 4, 3], F32)
        sm = sb("sm", [128, 288], F32)
        negA = sb("negA", [128, 4], F32)
        hbias = sb("hbias", [128, 4], F32)
        convh = sb("convh", [128, 16], F32)
        ps = [psum(f"ps{i}", [128, 512], F32) for i in range(7)]
        tb = psum("tb", [128, 8, 128], BF16)

        sems = {e: es.enter_context(nc.semaphore("s_" + e)) for e in Prog.ENGS}
        dsems = {e: [es.enter_context(nc.semaphore(f"d_{e}{i}")) for i in range(N_DMA_SEMS)] for e in Prog.ENGS}
        dsems["cc"] = es.enter_context(nc.semaphore("cc_sem"))
        P = Prog(nc)

        def pk(b, s0=0, n=4):
            return [f"p{b}"]

        def tbk(s0=0, n=8):
            return ["tb"]

        rr = [0]

        YKALL = [f"yT.r{c}" for c in range(4)] + [f"yT.s{c}" for c in range(4)] + ["yT.m0", "yT.m1", "yT.b0", "yT.b1"]
        HKALL = [f"hnT{c}" for c in range(4)]

        def dmaq():
            rr[0] += 1
            return DMA_QUEUES[rr[0] % len(DMA_QUEUES)]

        P.op("sp", lambda e: e.dma_start(out=xt[:, 0:640], in_=cf_d[:, 0:640]), writes=["xt"], dma=True)
        P.op("act", lambda e: e.dma_start(out=ox[:, 0:1024], in_=cf_d[:, 640:1664]), writes=["ox"], dma=True)
        P.op("dve", lambda e: e.tensor_copy(out=identb[:], in_=xt[:, 0:128]), reads=["xt"], writes=["identb"])
        P.op("dve", lambda e: e.tensor_copy(out=LEf[:], in_=xt[:, 128:256]), reads=["xt"], writes=["LEf"])
        P.op("dve", lambda e: e.tensor_copy(out=GEb[:], in_=xt[:, 256:384]), reads=["xt"], writes=["GEb"])
        P.op("dve", lambda e: e.tensor_copy(out=LTb[:], in_=xt[:, 384:512]), reads=["xt"], writes=["LTb"])
        P.op("dve", lambda e: e.tensor_copy(out=GTf[:], in_=xt[:, 512:640]), reads=["xt"], writes=["GTf"])
        P.op("dve", lambda e: e.tensor_copy(out=negm[:], in_=ox[:, 0:896]), reads=["ox"], writes=["negm"])
        P.op("dve", lambda e: e.tensor_copy(out=onesf[:], in_=ox[:, 896:1024]), reads=["ox"], writes=["onesf"])
        P.op("dve", lambda e: e.tensor_copy(out=onesb[:], in_=ox[:, 896:1024]), reads=["ox"], writes=["onesb"])
        def combine_rows(lp, rsl, ck, bufs=None):
            if bufs is None:
                bx, kx_, by, ky_ = xt, ["xt"], ox, ["ox"]
            else:
                bx, kx_, by, ky_ = bufs
            P.op("sp", lambda e: e.dma_start(out=bx, in_=Xd[lp][rsl, :]) if bufs else e.dma_start(out=bx[:], in_=Xd[lp][rsl, :]),
                 reads=([f"X{lp}.{ck}"] if lp > 0 else []), writes=kx_, dma=True)
            P.op("act", lambda e: e.dma_start(out=by, in_=Yd[lp][rsl, :]) if bufs else e.dma_start(out=by[:], in_=Yd[lp][rsl, :]),
                 reads=[f"Y{lp}.{ck}"], writes=ky_, dma=True)
            bxa = bx if bufs else bx[:]
            bya = by if bufs else by[:]
            P.op("act", lambda e: e.activation(out=hn[:], in_=bya, func=AF.Square, scale=1.0 / 32.0, accum_out=sm[:, 68:69]),
                 reads=ky_, writes=["hn", "sm68"])
            P.op("act", lambda e: e.activation(out=sm[:, 69:70], in_=sm[:, 68:69], func=AF.Ln, bias=EPS), reads=["sm68"], writes=["sm69"])
            P.op("act", lambda e: e.activation(out=sm[:, 70:71], in_=sm[:, 69:70], func=AF.Exp, scale=-0.5), reads=["sm69"], writes=["sm70"])
            P.op("dve", lambda e: e.scalar_tensor_tensor(out=bya, in0=bya, scalar=sm[:, 70:71], in1=rt[:], op0=ALU.mult, op1=ALU.mult),
                 reads=ky_ + ["sm70", "rt"], writes=ky_)
            P.op("pool", lambda e: e.tensor_tensor(out=bxa, in0=bxa, in1=bya, op=ALU.add), reads=kx_ + ky_, writes=kx_)

        for l in range(depth):
            if l == 0:
                P.op("sp", lambda e: e.dma_start(out=gains[:], in_=gains_ds[l]), writes=["gains"], dma=True)
            P.op("sp", lambda e: e.dma_start(out=convp[:], in_=convp_ds[l]), writes=["convp"], dma=True)
            P.op("sp", lambda e: e.dma_start(out=rowv[:], in_=rowv_ds[l].partition_broadcast(128)), writes=["rowv"], dma=True)
            P.op("dve", lambda e: e.tensor_scalar(out=hbias[:], in0=convp[:, 16:20], scalar1=0.5, scalar2=None, op0=ALU.mult),
                 reads=["convp"], writes=["hbias"])
            P.op("dve", lambda e: e.tensor_scalar(out=convh[:], in0=convp[:, 0:16], scalar1=0.5, scalar2=None, op0=ALU.mult),
                 reads=["convp"], writes=["convh"])
            P.op("act", lambda e: e.activation(out=negA[:], in_=rowv[:, 4:8], func=AF.Exp), reads=["rowv"], writes=["negA"])
            P.op("dve", lambda e: e.tensor_scalar(out=negA[:], in0=negA[:], scalar1=-1.0, scalar2=None, op0=ALU.mult),
                 reads=["negA"], writes=["negA"])
            P.op("pool", lambda e: e.memset(Rf[:], 0.0), writes=["Rf"])
            P.op("pool", lambda e: e.memset(Rb[:], 0.0), writes=["Rb"])
            P.op("pool", lambda e: e.memset(Sf[:], 0.0), writes=["Sf"])
            P.op("pool", lambda e: e.memset(Sb[:], 0.0), writes=["Sb"])
            P.op("pool", lambda e: e.memset(halo[:], 0.0), writes=["halo0", "halo1", "halo2", "halo3"])

            stg = [(xt, "xt"), (ox, "ox"), (rt, "rt")]
            si = [0]
            ceng = ["dve", "act"]

            def load_w(wd, Wt, wkey, ncols, gcol):
                for k in range(8):
                    c0 = 0
                    while c0 < ncols:
                        cw = min(1024, ncols - c0)
                        st, sk = stg[si[0] % 3]
                        ce = ceng[si[0] % len(ceng)]
                        si[0] += 1
                        q = dmaq()
                        P.op(q, lambda e, st=st, k=k, c0=c0, cw=cw: e.dma_start(out=st[:, 0:cw], in_=wd[k * 128:(k + 1) * 128, c0:c0 + cw]),
                             writes=[sk], dma=True)
                        if gcol is None:
                            if ce == "act":
                                f = lambda e, st=st, k=k, c0=c0, cw=cw: e.activation(out=Wt[:, k, c0:c0 + cw], in_=st[:, 0:cw], func=AF.Copy)
                            else:
                                f = lambda e, st=st, k=k, c0=c0, cw=cw: e.tensor_copy(out=Wt[:, k, c0:c0 + cw], in_=st[:, 0:cw])
                        else:
                            if ce == "act":
                                f = lambda e, st=st, k=k, c0=c0, cw=cw: e.activation(out=Wt[:, k, c0:c0 + cw], in_=st[:, 0:cw], func=AF.Copy,
                                                                                      scale=gains[:, gcol + k:gcol + k + 1])
                            else:
                                f = lambda e, st=st, k=k, c0=c0, cw=cw: e.tensor_scalar(out=Wt[:, k, c0:c0 + cw], in0=st[:, 0:cw],
                                                                                        scalar1=gains[:, gcol + k:gcol + k + 1], scalar2=None, op0=ALU.mult)
                        P.op(ce, f, reads=[sk, "gains"], writes=(YKALL if wkey == "Wmem" else [wkey]))
                        c0 += cw

            load_w(wmem_ds[l], Wmem, "Wmem", 512, 8)
            if l == 0:
                load_w(wfm_ds[l], Wfm, "Wfm", NFM, 0)
                load_w(wtm_ds[l], Wtm, "Wtm", NTM, 0)
            load_w(wout_ds[l], Wout, "Wout", D, None)

            def norm_rows(src_ap):
                if src_ap is not None:
                    P.op("sp", lambda e: e.dma_start(out=xt[:], in_=src_ap), writes=["xt"], dma=True)
                P.op("act", lambda e: e.activation(out=hn[:], in_=xt[:], func=AF.Square, scale=1.0 / 32.0, accum_out=sm[:, 0:1]),
                     reads=["xt"], writes=["hn", "sm0"])
                P.op("act", lambda e: e.activation(out=sm[:, 1:2], in_=sm[:, 0:1], func=AF.Ln, bias=EPS), reads=["sm0"], writes=["sm1"])
                P.op("act", lambda e: e.activation(out=sm[:, 2:3], in_=sm[:, 1:2], func=AF.Exp, scale=-0.5), reads=["sm1"], writes=["sm2"])
                P.op("dve", lambda e: e.tensor_scalar(out=hn[:], in0=xt[:], scalar1=sm[:, 2:3], scalar2=None, op0=ALU.mult),
                     reads=["xt", "sm2"], writes=["hn"])

            for mc in range(2):
                norm_rows(mem_d[mc * 128:(mc + 1) * 128, :])
                for k in range(8):
                    P.op("pe", lambda e, k=k: e.transpose(out=tb[:, k, :], in_=hn[:, k * 128:(k + 1) * 128], identity=identb[:]),
                         reads=["hn", "identb"], writes=["tb"])
                P.op("dve", lambda e, mc=mc: e.tensor_copy(out=memT[:, :, mc * 128:(mc + 1) * 128], in_=tb[:]), reads=tbk(), writes=HKALL)
            for h in range(2):
                for k in range(8):
                    P.op("pe", lambda e, h=h, k=k: e.matmul(ps[0][:, 0:256], lhsT=Wmem[:, k, h * 128:(h + 1) * 128], rhs=memT[:, k, 0:256],
                                                             start=(k == 0), stop=(k == 7)), reads=YKALL + HKALL, writes=pk(0, 0, 2))
                P.op("dve", lambda e, h=h: e.tensor_copy(out=kmT[:, h, :], in_=ps[0][:, 0:256]), reads=pk(0, 0, 2), writes=["kmT"])
            for mb in range(2):
                for k in range(8):
                    P.op("pe", lambda e, mb=mb, k=k: e.matmul(ps[1][:, 0:256], lhsT=memT[:, k, mb * 128:(mb + 1) * 128], rhs=Wmem[:, k, 256:512],
                                                               start=(k == 0), stop=(k == 7)), reads=YKALL + HKALL, writes=pk(1, 0, 2))
                P.op("dve", lambda e, mb=mb: e.tensor_copy(out=vm[:, mb, :], in_=ps[1][:, 0:256]), reads=pk(1, 0, 2), writes=["vm"])

            F0, F1, F2, F3, F4 = Fs
            H0, H1, H2, H3, H4, H5, H6, H7 = Hs

            g128 = [rowv[:, 8:9], rowv[:, 9:10]]
            dsk = rowv[:, 16:272]
            sng = rowv[:, 272:528]

            for g in range(NG):
                t0 = g * 512
                def emit_A0(g):
                    t0 = g * 512
                    if l > 0:
                        P.op("act", lambda e: e.dma_start(out=rt[:], in_=postg_ds[l - 1].partition_broadcast(128)), writes=["rt"], dma=True)
                    for c in range(4):
                        rsl = slice(t0 + c * 128, t0 + (c + 1) * 128)
                        if l == 0:
                            norm_rows(x_d[rsl, :])
                        else:
                            combine_rows(l - 1, rsl, g * 4 + c)
                            P.op("sp", lambda e: e.dma_start(out=Xd[l][rsl, :], in_=xt[:]), reads=["xt"], writes=[f"X{l}.{g * 4 + c}"], dma=True)
                            norm_rows(None)
                        for k in range(8):
                            P.op("pe", lambda e, k=k: e.transpose(out=tb[:, k, :], in_=hn[:, k * 128:(k + 1) * 128], identity=identb[:]),
                                 reads=["hn", "identb"], writes=["tb"])
                        P.op("dve", lambda e, c=c: e.tensor_copy(out=hnT[:, :, c * 128:(c + 1) * 128], in_=tb[:]), reads=tbk(), writes=[f"hnT{c}"])
                if g == 0:
                    emit_A0(0)
                def emit_A1(g):
                    t0 = g * 512
                    hnTk = [f"hnT{c}" for c in range(4)]
                    def fm_block(j, bank):
                        for k in range(8):
                            P.op("pe", lambda e, k=k: e.matmul(ps[bank][:], lhsT=Wfm[:, k, j * 128:(j + 1) * 128], rhs=hnT[:, k, :],
                                                               start=(k == 0), stop=(k == 7)), reads=["Wfm"] + hnTk, writes=pk(bank))

                    sc = 1.0 / math.sqrt(128.0)
                    for jj, j in enumerate([10, 11, 12, 13, 0, 1, 2, 3, 4, 5, 6, 7, 8, 9]):
                        bank = jj % 4
                        fm_block(j, bank)
                        src = ps[bank]
                        rk = pk(bank)
                        if j < 2:
                            P.op("dve", lambda e, j=j, src=src: e.tensor_scalar(out=qT[:, j, :], in0=src[:], scalar1=sc, scalar2=None, op0=ALU.mult),
                                 reads=rk, writes=[f"qT{j}"])
                        elif j < 4:
                            h = j - 2
                            P.op("dve", lambda e, h=h, src=src: e.tensor_copy(out=KT[:, h, t0:t0 + 512], in_=src[:]), reads=rk, writes=[f"KT{h}.{g}"])
                        elif j < 6:
                            h = j - 4
                            P.op("act", lambda e, src=src: e.activation(out=F4[:, 0:512], in_=src[:], func=AF.Tanh, scale=0.5), reads=rk, writes=["F4"])
                            P.op("dve", lambda e, h=h, src=src: e.scalar_tensor_tensor(out=gsb[:, h, :], in0=F4[:, 0:512], scalar=1.0, in1=src[:],
                                                                                        op0=ALU.add, op1=ALU.mult), reads=rk + ["F4"], writes=[f"gsb{h}"])
                        elif j < 8:
                            h = j - 6
                            P.op("dve", lambda e, h=h, src=src: e.tensor_scalar(out=qmT[:, h, :], in0=src[:], scalar1=sc, scalar2=None, op0=ALU.mult),
                                 reads=rk, writes=[f"qmT{h}"])
                        elif j < 10:
                            h = j - 8
                            P.op("act", lambda e, src=src: e.activation(out=F4[:, 0:512], in_=src[:], func=AF.Tanh, scale=0.5), reads=rk, writes=["F4"])
                            P.op("dve", lambda e, h=h, src=src: e.scalar_tensor_tensor(out=gm[:, h, :], in0=F4[:, 0:512], scalar=1.0, in1=src[:],
                                                                                        op0=ALU.add, op1=ALU.mult), reads=rk + ["F4"], writes=[f"gm{h}"])
                        else:
                            cb = j - 10
                            Fr, Fa = ((F0, F1), (F2, F3))[cb % 2]
                            kr, ka = (("F0", "F1"), ("F2", "F3"))[cb % 2]
                            P.op("pool", lambda e, cb=cb: e.tensor_copy(out=Fr[:, 0:3], in_=halo[:, cb, :]), reads=[f"halo{cb}"], writes=[kr])
                            P.op("dve", lambda e, src=src: e.tensor_copy(out=Fr[:, 3:515], in_=src[:]), reads=rk, writes=[kr])
                            P.op("pool", lambda e, cb=cb: e.tensor_copy(out=halo[:, cb, :], in_=Fr[:, 512:515]), reads=[kr], writes=[f"halo{cb}"])
                            P.op("pool", lambda e, cb=cb: e.tensor_scalar(out=Fa[:, 0:512], in0=Fr[:, 0:512], scalar1=convh[:, cb * 4:cb * 4 + 1],
                                                                           scalar2=hbias[:, cb:cb + 1], op0=ALU.mult, op1=ALU.add),
                                 reads=[kr, "convh", "hbias"], writes=[ka])
                            for i in range(1, 4):
                                P.op("dve", lambda e, cb=cb, i=i: e.scalar_tensor_tensor(out=Fa[:, 0:512], in0=Fr[:, i:i + 512],
                                                                                           scalar=convh[:, cb * 4 + i:cb * 4 + i + 1], in1=Fa[:, 0:512],
                                                                                           op0=ALU.mult, op1=ALU.add), reads=[kr, ka, "convh"], writes=[ka])
                            P.op("act", lambda e: e.activation(out=Fr[:, 0:512], in_=Fa[:, 0:512], func=AF.Tanh), reads=[ka], writes=[kr])
                            P.op("dve", lambda e, cb=cb: e.scalar_tensor_tensor(out=xcT[:, cb, :], in0=Fr[:, 0:512], scalar=1.0, in1=Fa[:, 0:512],
                                                                                 op0=ALU.add, op1=ALU.mult), reads=[ka, kr], writes=[f"xcT{cb}"])
                if g == 0:
                    emit_A1(0)
                for c in range(4):
                    for k in range(8):
                        P.op("pe", lambda e, k=k, c=c: e.matmul(ps[5][:, c * 4:(c + 1) * 4], lhsT=hnT[:, k, c * 128:(c + 1) * 128], rhs=Wtm[:, k, 1536:1540],
                                                                 start=(k == 0), stop=(k == 7)), reads=["Wtm", f"hnT{c}"], writes=pk(5))
                v44 = lambda ap: ap.rearrange("p (c h) -> p c h", c=4)
                P.op("dve", lambda e: e.tensor_tensor(out=v44(sm[:, 100:116]), in0=v44(ps[5][:, 0:16]),
                                                      in1=rowv[:, 0:4].unsqueeze(1).to_broadcast([128, 4, 4]), op=ALU.add),
                     reads=pk(5) + ["rowv"], writes=["smB0"])
                P.op("act", lambda e: e.activation(out=sm[:, 116:132], in_=sm[:, 100:116], func=AF.Exp), reads=["smB0"], writes=["smB1"])
                P.op("act", lambda e: e.activation(out=sm[:, 132:148], in_=sm[:, 116:132], func=AF.Ln, bias=1.0), reads=["smB1"], writes=["smB"])
                P.op("dve", lambda e: e.tensor_tensor(out=v44(sm[:, 148:164]), in0=v44(sm[:, 132:148]),
                                                      in1=negA[:].unsqueeze(1).to_broadcast([128, 4, 4]), op=ALU.mult),
                     reads=["smB", "negA"], writes=["smB"])
                P.op("pe", lambda e: e.matmul(ps[1][:, 384:400], lhsT=LEf[:], rhs=sm[:, 148:164], start=True, stop=True), reads=["LEf", "smB"], writes=pk(1))
                P.op("pe", lambda e: e.matmul(ps[1][:, 400:416], lhsT=onesf[:], rhs=sm[:, 148:164], start=True, stop=True), reads=["onesf", "smB"], writes=pk(1))
                P.op("dve", lambda e: e.tensor_copy(out=sm[:, 164:196], in_=ps[1][:, 384:416]), reads=pk(1), writes=["smB"])
                P.op("act", lambda e: e.activation(out=sm[:, 196:228], in_=sm[:, 164:196], func=AF.Exp), reads=["smB"], writes=["smB"])
                P.op("dve", lambda e: e.tensor_tensor(out=sm[:, 228:244], in0=sm[:, 180:196], in1=sm[:, 164:180], op=ALU.subtract), reads=["smB"], writes=["smB"])
                P.op("act", lambda e: e.activation(out=sm[:, 244:260], in_=sm[:, 228:244], func=AF.Exp), reads=["smB"], writes=["smB"])
                P.op("dve", lambda e: e.tensor_tensor(out=sm[:, 260:276], in0=sm[:, 244:260], in1=sm[:, 132:148], op=ALU.mult), reads=["smB"], writes=["smB"])
                def emit_TM(c):
                    csl = slice(c * 128, (c + 1) * 128)
                    for (bank, c0, cw) in ((2, 0, 512), (3, 512, 512), (4, 1024, 512)):
                        for k in range(8):
                            P.op("pe", lambda e, k=k, bank=bank, c0=c0, cw=cw: e.matmul(ps[bank][:, 0:cw], lhsT=hnT[:, k, csl], rhs=Wtm[:, k, c0:c0 + cw],
                                                                                         start=(k == 0), stop=(k == 7)),
                                 reads=["Wtm", f"hnT{c}"], writes=pk(bank))

                for c in range(4):
                    cc = g * 4 + c
                    cs = slice(c * 128, (c + 1) * 128)
                    P.op("act", lambda e, cc=cc: e.dma_start(out=rt[:], in_=rtab_d[cc]), writes=["rt"], dma=True)
                    if c == 0:
                        emit_TM(0)
                    P.capture_begin()
                    qk4 = ps[2][:].rearrange("p (w h d) -> p w h d", w=4, h=2)
                    rtS = rt[:, 512:1024].rearrange("p (w h d) -> p w h d", w=4, h=2)
                    F1v = F1[:, 0:512].rearrange("p (w h d) -> p w h d", w=4, h=2)
                    P.op("dve", lambda e: e.tensor_tensor(out=F0[:, 0:512], in0=ps[2][:], in1=rt[:, 0:512], op=ALU.mult),
                         reads=pk(2) + ["rt"], writes=["F0"])
                    P.op("dve", lambda e: e.tensor_tensor(out=F1v[:, :, 0, :], in0=qk4[:, :, 1, :], in1=rtS[:, :, 0, :], op=ALU.mult),
                         reads=pk(2) + ["rt"], writes=["F1"])
                    P.op("dve", lambda e: e.tensor_tensor(out=F1v[:, :, 1, :], in0=qk4[:, :, 0, :], in1=rtS[:, :, 1, :], op=ALU.mult),
                         reads=pk(2) + ["rt"], writes=["F1"])
                    P.op("pool", lambda e: e.tensor_tensor(out=H0[:], in0=F0[:, 0:512], in1=F1[:, 0:512], op=ALU.add),
                         reads=["F0", "F1"], writes=["H0"])
                    P.op("dve", lambda e: e.tensor_copy(out=H2[:, 0:256], in_=ps[3][:, 0:256]), reads=pk(3, 0, 2), writes=["H2a"])
                    P.op("act", lambda e: e.activation(out=F1[:, 0:256], in_=ps[3][:, 256:512], func=AF.Exp, scale=-1.0), reads=pk(3, 2, 2), writes=["F1"])
                    P.op("act", lambda e: e.activation(out=F1[:, 256:512], in_=F1[:, 0:256], func=AF.Ln, bias=1.0), reads=["F1"], writes=["F1"])
                    P.op("act", lambda e: e.activation(out=F1[:, 0:256], in_=F1[:, 256:512], func=AF.Exp, scale=-1.0), reads=["F1"], writes=["F1"])
                    P.op("dve", lambda e: e.tensor_tensor(out=H2[:, 256:512], in0=ps[3][:, 256:512], in1=F1[:, 0:256], op=ALU.mult),
                         reads=pk(3, 2, 2) + ["F1"], writes=["H2b"])
                    P.op("dve", lambda e, cc=cc: e.tensor_copy(out=VV[:, :, cc, :], in_=ps[4][:, 0:256].rearrange("p (h d) -> p h d", h=2)),
                         reads=pk(4, 0, 2), writes=[f"VV{cc}"])
                    P.op("dve", lambda e: e.tensor_copy(out=H7[:, 256:512], in_=ps[4][:, 256:512]), reads=pk(4), writes=["H7.2", "H7.3"])
                    npre = P.mark()
                    for w in range(4):
                        P.op("pe", lambda e, w=w: e.transpose(out=tb[:, w, :], in_=H0[:, w * 128:(w + 1) * 128], identity=identb[:]),
                             reads=["H0", "identb"], writes=["tb"])
                    P.op("dve", lambda e: e.tensor_copy(out=H1[:].rearrange("p (w d) -> p w d", w=4), in_=tb[:, 0:4, :]), reads=tbk(0, 4), writes=["H1"])
                    for h in range(2):
                        P.op("pe", lambda e, h=h: e.matmul(ps[5][:, h * 128:(h + 1) * 128], lhsT=H1[:, (2 + h) * 128:(3 + h) * 128],
                                                            rhs=H1[:, h * 128:(h + 1) * 128], start=True, stop=True),
                             reads=["H1"], writes=pk(5, h, 1))
                        P.op("dve", lambda e, h=h: e.tensor_tensor(out=H3[:, h * 128:(h + 1) * 128], in0=ps[5][:, h * 128:(h + 1) * 128],
                                                                    in1=LEf[:], op=ALU.mult), reads=pk(5, h, 1) + ["LEf"], writes=[f"H3.{h}"])
                        P.op("pe", lambda e, h=h: e.matmul(ps[6][:, h * 128:(h + 1) * 128], lhsT=H3[:, h * 128:(h + 1) * 128],
                                                            rhs=H2[:, h * 128:(h + 1) * 128], start=True, stop=False),
                             reads=[f"H3.{h}", "H2a"], writes=pk(6, h, 1))
                        P.op("pe", lambda e, h=h: e.matmul(ps[6][:, h * 128:(h + 1) * 128], lhsT=H1[:, h * 128:(h + 1) * 128],
                                                            rhs=Rb[:, h, :], start=False, stop=True),
                             reads=["H1", "Rb"], writes=pk(6, h, 1))
                        P.op("pe", lambda e, h=h: e.matmul(ps[6][:, 256 + h * 128:384 + h * 128], lhsT=H0[:, (2 + h) * 128:(3 + h) * 128],
                                                            rhs=H2[:, h * 128:(h + 1) * 128], start=True, stop=True),
                             reads=["H0", "H2a"], writes=pk(6, 2 + h, 1))
                        P.op("dve", lambda e, h=h: e.scalar_tensor_tensor(out=Rf[:, h, :], in0=Rf[:, h, :], scalar=g128[h],
                                                                           in1=ps[6][:, 256 + h * 128:384 + h * 128], op0=ALU.mult, op1=ALU.add),
                             reads=["Rf", "rowv"] + pk(6, 2 + h, 1), writes=["Rf"])
                        P.op("act", lambda e, h=h: e.activation(out=Rb[:, h, :], in_=Rf[:, h, :], func=AF.Copy, scale=g128[h]),
                             reads=["Rf", "rowv"], writes=["Rb"])
                    y3 = ps[6][:, 0:256].rearrange("p (h d) -> p h d", h=2)
                    P.op("dve", lambda e: e.tensor_reduce(out=sm[:, 4:6], in_=y3, axis=AX.X, op=ALU.add), reads=pk(6, 0, 2), writes=["sm4"])
                    for h in range(2):
                        P.op("act", lambda e, h=h: e.activation(out=F2[:, h * 128:(h + 1) * 128], in_=ps[6][:, h * 128:(h + 1) * 128], func=AF.Square,
                                                                 accum_out=sm[:, 6 + h:7 + h]), reads=pk(6, h, 1), writes=["F2", f"sm6{h}"])
                    P.op("dve", lambda e: e.tensor_scalar(out=sm[:, 8:10], in0=sm[:, 4:6], scalar1=1.0 / 128, scalar2=None, op0=ALU.mult),
                         reads=["sm4"], writes=["sm8"])
                    P.op("dve", lambda e: e.tensor_tensor(out=sm[:, 10:12], in0=sm[:, 8:10], in1=sm[:, 8:10], op=ALU.mult), reads=["sm8"], writes=["sm10"])
                    P.op("dve", lambda e: e.scalar_tensor_tensor(out=sm[:, 12:14], in0=sm[:, 6:8], scalar=1.0 / 128, in1=sm[:, 10:12],
                                                                 op0=ALU.mult, op1=ALU.subtract), reads=["sm60", "sm61", "sm10"], writes=["sm12"])
                    P.op("act", lambda e: e.activation(out=sm[:, 14:16], in_=sm[:, 12:14], func=AF.Ln, bias=EPS), reads=["sm12"], writes=["sm14"])
                    P.op("act", lambda e: e.activation(out=sm[:, 16:18], in_=sm[:, 14:16], func=AF.Exp, scale=-0.5), reads=["sm14"], writes=["sm16"])
                    for h in range(2):
                        P.op("dve", lambda e, h=h: e.tensor_scalar(out=F2[:, h * 128:(h + 1) * 128], in0=ps[6][:, h * 128:(h + 1) * 128],
                                                                    scalar1=sm[:, 8 + h:9 + h], scalar2=sm[:, 16 + h:17 + h],
                                                                    op0=ALU.subtract, op1=ALU.mult), reads=pk(6, h, 1) + ["sm8", "sm16"], writes=["F2"])
                    P.op("pool", lambda e: e.tensor_tensor(out=H3[:, 256:512], in0=F2[:, 0:256], in1=H2[:, 256:512], op=ALU.mult),
                         reads=["F2", "H2b"], writes=["H3c"])
                    for h in range(2):
                        P.op("pe", lambda e, h=h: e.transpose(out=tb[:, 4 + h, :], in_=H3[:, 256 + h * 128:384 + h * 128], identity=identb[:]),
                             reads=["H3c", "identb"], writes=["tb"])
                    P.op("dve", lambda e: e.tensor_copy(out=yT[:, 0:2, cs], in_=tb[:, 4:6, :]), reads=tbk(4, 2), writes=[f"yT.r{c}"])
                    cap_ret = P.capture_end()
                    P.capture_begin()
                    o4 = 4 * c
                    for h in range(4):
                        P.op("dve", lambda e, h=h: e.tensor_scalar(out=F3[:, h * 128:(h + 1) * 128], in0=GTf[:], scalar1=sm[:, 148 + o4 + h:149 + o4 + h],
                                                                    scalar2=None, op0=ALU.mult), reads=["GTf", "smB"], writes=[f"F3.{h}"])
                        P.op("pe", lambda e, h=h: e.matmul(ps[0][:, h * 128:(h + 1) * 128], lhsT=F3[:, h * 128:(h + 1) * 128], rhs=LEf[:],
                                                            start=True, stop=True), reads=[f"F3.{h}", "LEf"], writes=pk(0, h, 1))
                        P.op("act", lambda e, h=h: e.activation(out=F4[:, h * 128:(h + 1) * 128], in_=ps[0][:, h * 128:(h + 1) * 128], func=AF.Exp),
                             reads=pk(0, h, 1), writes=[f"F4.{h}"])
                    P.op("pe", lambda e: e.matmul(ps[1][:, 0:128], lhsT=xcT[:, 2, cs], rhs=xcT[:, 3, cs], start=True, stop=True),
                         reads=["xcT2", "xcT3"], writes=pk(1, 0, 1))
                    P.op("dve", lambda e: e.tensor_tensor(out=F2[:, 256:384], in0=ps[1][:, 0:128], in1=LEf[:], op=ALU.mult),
                         reads=pk(1, 0, 1) + ["LEf"], writes=["F2c"])
                    for h in range(4):
                        P.op("pool", lambda e, h=h: e.tensor_tensor(out=H4[:, h * 128:(h + 1) * 128], in0=F2[:, 256:384], in1=F4[:, h * 128:(h + 1) * 128],
                                                                     op=ALU.mult), reads=["F2c", f"F4.{h}"], writes=[f"H4.{h}"])
                    for i in range(3):
                        P.op("pe", lambda e, i=i: e.transpose(out=tb[:, i, :], in_=xcT[:, i, cs], identity=identb[:]),
                             reads=[f"xcT{i}", "identb"], writes=["tb"])
                    P.op("dve", lambda e: e.tensor_copy(out=H5[:, 0:384].rearrange("p (w d) -> p w d", w=3), in_=tb[:, 0:3, :]), reads=tbk(0, 3), writes=["H5"])
                    xs3 = H5[:, 0:256].rearrange("p (h d) -> p h d", h=4)
                    P.op("dve", lambda e: e.tensor_tensor(out=H6[:, 0:256].rearrange("p (h d) -> p h d", h=4), in0=xs3,
                                                          in1=sm[:, 132 + o4:136 + o4].unsqueeze(2).to_broadcast([128, 4, 64]), op=ALU.mult),
                         reads=["H5", "smB"], writes=["H6a"])
                    P.op("dve", lambda e: e.tensor_tensor(out=H6[:, 256:512].rearrange("p (h d) -> p h d", h=4), in0=xs3,
                                                          in1=sm[:, 260 + o4:264 + o4].unsqueeze(2).to_broadcast([128, 4, 64]), op=ALU.mult),
                         reads=["H5", "smB"], writes=["H6b"])
                    for h in range(4):
                        P.op("pe", lambda e, h=h: e.matmul(ps[1][:, 128 + h * 64:192 + h * 64], lhsT=H4[:, h * 128:(h + 1) * 128],
                                                            rhs=H6[:, h * 64:(h + 1) * 64], start=True, stop=True),
                             reads=[f"H4.{h}", "H6a"], writes=pk(1, 1, 2))
                    P.op("pe", lambda e: e.matmul(ps[0][:, 0:256], lhsT=xcT[:, 3, cs], rhs=Sb[:], start=True, stop=True),
                         reads=["xcT3", "Sb"], writes=pk(0, 0, 2))
                    P.op("pe", lambda e: e.matmul(ps[0][:, 256:512], lhsT=H5[:, 256:384], rhs=H6[:, 256:512], start=True, stop=True),
                         reads=["H5", "H6b"], writes=pk(0, 2, 2))
                    P.op("dve", lambda e: e.tensor_tensor(out=F3[:, 0:256].rearrange("p (h d) -> p h d", h=4),
                                                          in0=ps[0][:, 0:256].rearrange("p (h d) -> p h d", h=4),
                                                          in1=sm[:, 196 + o4:200 + o4].unsqueeze(2).to_broadcast([128, 4, 64]), op=ALU.mult),
                         reads=pk(0, 0, 2) + ["smB"], writes=["F3.0", "F3.1"])
                    P.op("dve", lambda e: e.tensor_tensor(out=F3[:, 0:256], in0=F3[:, 0:256], in1=ps[1][:, 128:384], op=ALU.add),
                         reads=["F3.0", "F3.1"] + pk(1, 1, 2), writes=["F3.0", "F3.1"])
                    P.op("pool", lambda e: e.tensor_tensor(out=F3[:, 256:512], in0=H5[:, 0:256], in1=dsk, op=ALU.mult), reads=["H5", "rowv"], writes=["F3.2", "F3.3"])
                    P.op("pool", lambda e: e.tensor_tensor(out=F3[:, 0:256], in0=F3[:, 0:256], in1=F3[:, 256:512], op=ALU.add), reads=["F3.0", "F3.1", "F3.2", "F3.3"], writes=["F3.0", "F3.1"])
                    P.op("dve", lambda e: e.tensor_tensor(out=Sf[:].rearrange("p (h d) -> p h d", h=4), in0=Sf[:].rearrange("p (h d) -> p h d", h=4),
                                                          in1=sm[:, 212 + o4:216 + o4].unsqueeze(2).to_broadcast([128, 4, 64]), op=ALU.mult),
                         reads=["Sf", "smB"], writes=["Sf"])
                    P.op("dve", lambda e: e.tensor_tensor(out=Sf[:], in0=Sf[:], in1=ps[0][:, 256:512], op=ALU.add), reads=["Sf"] + pk(0, 2, 2), writes=["Sf"])
                    P.op("act", lambda e: e.activation(out=Sb[:], in_=Sf[:], func=AF.Copy), reads=["Sf"], writes=["Sb"])
                    P.op("act", lambda e: e.activation(out=F4[:, 256:512], in_=H7[:, 256:512], func=AF.Exp, scale=-1.0), reads=["H7.2", "H7.3"], writes=["F4.2", "F4.3"])
                    P.op("act", lambda e: e.activation(out=F4[:, 0:256], in_=F4[:, 256:512], func=AF.Ln, bias=1.0), reads=["F4.2", "F4.3"], writes=["F4.0", "F4.1"])
                    P.op("act", lambda e: e.activation(out=F4[:, 256:512], in_=F4[:, 0:256], func=AF.Exp, scale=-1.0), reads=["F4.0", "F4.1"], writes=["F4.2", "F4.3"])
                    P.op("dve", lambda e: e.tensor_tensor(out=F4[:, 0:256], in0=H7[:, 256:512], in1=F4[:, 256:512], op=ALU.mult),
                         reads=["H7.2", "H7.3", "F4.2", "F4.3"], writes=["F4.0", "F4.1"])
                    P.op("pool", lambda e: e.tensor_tensor(out=F3[:, 0:256], in0=F3[:, 0:256], in1=F4[:, 0:256], op=ALU.mult), reads=["F3.0", "F3.1", "F4.0", "F4.1"], writes=["F3.0", "F3.1"])
                    P.op("act", lambda e: e.activation(out=F4[:, 0:256], in_=F3[:, 0:256], func=AF.Square, scale=1.0 / 16.0, accum_out=sm[:, 64:65]),
                         reads=["F3.0", "F3.1"], writes=["F4.0", "F4.1", "sm64"])
                    P.op("act", lambda e: e.activation(out=sm[:, 65:66], in_=sm[:, 64:65], func=AF.Ln, bias=EPS), reads=["sm64"], writes=["sm65"])
                    P.op("act", lambda e: e.activation(out=sm[:, 66:67], in_=sm[:, 65:66], func=AF.Exp, scale=-0.5), reads=["sm65"], writes=["sm66"])
                    P.op("dve", lambda e: e.scalar_tensor_tensor(out=H7[:, 0:256], in0=F3[:, 0:256], scalar=sm[:, 66:67], in1=sng,
                                                                 op0=ALU.mult, op1=ALU.mult), reads=["F3.0", "F3.1", "sm66", "rowv"], writes=["H7.0", "H7.1"])
                    for i in range(2):
                        P.op("pe", lambda e, i=i: e.transpose(out=tb[:, 6 + i, :], in_=H7[:, i * 128:(i + 1) * 128], identity=identb[:]),
                             reads=["H7.0", "H7.1", "identb"], writes=["tb"])
                    P.op("dve", lambda e: e.tensor_copy(out=yT[:, 2:4, cs], in_=tb[:, 6:8, :]), reads=tbk(6, 2), writes=[f"yT.s{c}"])
                    cap_ssd = P.capture_end()
                    cap_tm = []
                    if c + 1 < 4:
                        P.capture_begin()
                        emit_TM(c + 1)
                        cap_tm = P.capture_end()
                    P.register(cap_ret[:npre])
                    M = P.interleave(cap_ret[npre:], cap_ssd)
                    k0 = len(M) // 4
                    P.register(M[:k0])
                    P.register(P.interleave(M[k0:], cap_tm))
                for h in range(2):
                    for mb in range(2):
                        P.op("pe", lambda e, h=h, mb=mb: e.matmul(ps[2 + mb][:], lhsT=kmT[:, h, mb * 128:(mb + 1) * 128], rhs=qmT[:, h, :],
                                                                   start=True, stop=True), reads=["kmT", f"qmT{h}"], writes=pk(2 + mb))
                        P.op("act", lambda e, mb=mb: e.activation(out=Hs[mb][:], in_=ps[2 + mb][:], func=AF.Exp), reads=pk(2 + mb), writes=[f"H{mb}"])
                    for mb in range(2):
                        P.op("pe", lambda e, mb=mb: e.matmul(ps[4][:], lhsT=onesb[:], rhs=Hs[mb][:], start=(mb == 0), stop=(mb == 1)),
                             reads=["onesb", f"H{mb}"], writes=pk(4))
                    for mb in range(2):
                        P.op("pe", lambda e, h=h, mb=mb: e.matmul(ps[5][:], lhsT=vm[:, mb, h * 128:(h + 1) * 128], rhs=Hs[mb][:],
                                                                   start=(mb == 0), stop=(mb == 1)), reads=["vm", f"H{mb}"], writes=pk(5))
                    P.op("dve", lambda e: e.reciprocal(out=F0[:, 0:512], in_=ps[4][:]), reads=pk(4), writes=["F0"])
                    P.op("dve", lambda e: e.scalar_tensor_tensor(out=F1[:, 0:512], in0=ps[5][:], scalar=0.5, in1=F0[:, 0:512], op0=ALU.mult, op1=ALU.mult),
                         reads=pk(5) + ["F0"], writes=["F1"])
                    P.op("pool", lambda e, h=h: e.tensor_tensor(out=yT[:, 6 + h, :], in0=F1[:, 0:512], in1=gm[:, h, :], op=ALU.mult),
                         reads=["F1", f"gm{h}"], writes=[f"yT.m{h}"])
                P.capture_begin()
                for h in range(2):
                    units = [("d", j) for j in (3, 2, 1, 0)] + [("o", kb) for kb in range(4 * g - 1, -1, -1)]
                    n = len(units)
                    E = [F0, F1, F2, F3]
                    Gt = [F4, F4]
                    SP = [H0, H1]
                    Wb = [H2, H3]
                    SS = [H4, H5]
                    CB = [2, 4]

                    def qk(i):
                        kind, v = units[i]
                        kb = 4 * g + v if kind == "d" else v
                        gi = kb // 4
                        bank = i % 2
                        P.op("pe", lambda e: e.matmul(ps[bank][:], lhsT=KT[:, h, kb * 128:(kb + 1) * 128], rhs=qT[:, h, :],
                                                      start=True, stop=(kind != "d")), reads=[f"KT{h}.{gi}", f"qT{h}"], writes=pk(bank))
                        if kind == "d":
                            o0 = 384 - 128 * v
                            P.op("pe", lambda e: e.matmul(ps[bank][:], lhsT=identb[:], rhs=negm[:, o0:o0 + 512], start=False, stop=True),
                                 reads=["identb", "negm"], writes=pk(bank))

                    def e_(i):
                        P.op("act", lambda e: e.activation(out=E[i % 4][:, 0:512], in_=ps[i % 2][:], func=AF.Exp), reads=pk(i % 2), writes=[f"F{i % 4}"])

                    def sp_(i):
                        P.op("act", lambda e: e.activation(out=SP[i % 2][:], in_=E[i % 4][:, 0:512], func=AF.Ln, bias=1.0),
                             reads=[f"F{i % 4}"], writes=[f"H{i % 2}"])

                    def cmm(i):
                        cb = CB[i % 2]
                        P.op("pe", lambda e: e.matmul(ps[cb][:], lhsT=GEb[:], rhs=SP[i % 2][:], start=True, stop=(i == 0)),
                             reads=["GEb", f"H{i % 2}"], writes=pk(cb))
                        if i > 0:
                            P.op("pe", lambda e: e.matmul(ps[cb][:], lhsT=onesb[:], rhs=SS[(i - 1) % 2][:], start=False, stop=True),
                                 reads=["onesb", f"H{4 + (i - 1) % 2}"], writes=pk(cb))

                    def g_(i):
                        cb = CB[i % 2]
                        P.op("act", lambda e: e.activation(out=Gt[i % 2][:, 0:512], in_=ps[cb][:], func=AF.Exp, scale=-1.0),
                             reads=pk(cb), writes=["F4"])

                    def w_(i):
                        P.op("dve", lambda e: e.tensor_tensor(out=Wb[i % 2][:], in0=E[i % 4][:, 0:512], in1=Gt[i % 2][:, 0:512], op=ALU.mult),
                             reads=[f"F{i % 4}", "F4"], writes=[f"H{2 + i % 2}"])

                    def ss_(i):
                        if i == 0:
                            P.op("pool", lambda e: e.tensor_copy(out=SS[0][:], in_=SP[0][:]), reads=["H0"], writes=["H4"])
                        else:
                            P.op("pool", lambda e: e.tensor_tensor(out=SS[i % 2][:], in0=SS[(i - 1) % 2][:], in1=SP[i % 2][:], op=ALU.add),
                                 reads=[f"H{4 + (i - 1) % 2}", f"H{i % 2}"], writes=[f"H{4 + i % 2}"])

                    def pv(i):
                        kind, v = units[i]
                        kb = 4 * g + v if kind == "d" else v
                        P.op("pe", lambda e: e.matmul(ps[3][:], lhsT=VV[:, h, kb, :], rhs=Wb[i % 2][:], start=(i == 0), stop=(i == n - 1)),
                             reads=[f"VV{kb}", f"H{2 + i % 2}"], writes=pk(3))

                    qk(0)
                    if n > 1:
                        qk(1)
                    e_(0)
                    for i in range(n + 2):
                        if i < n:
                            sp_(i)
                            cmm(i)
                            if i + 1 < n:
                                e_(i + 1)
                            if i + 2 < n:
                                qk(i + 2)
                        if 1 <= i <= n:
                            g_(i - 1)
                            w_(i - 1)
                        if i + 1 < n:
                            ss_(i)
                        if 2 <= i:
                            pv(i - 2)
                    P.op("dve", lambda e, h=h: e.scalar_tensor_tensor(out=yT[:, 4 + h, :], in0=ps[3][:], scalar=0.5, in1=gsb[:, h, :],
                                                                       op0=ALU.mult, op1=ALU.mult), reads=pk(3) + [f"gsb{h}"], writes=[f"yT.b{h}"])
                capB = P.capture_end()
                capA = []
                if g + 1 < NG:
                    P.capture_begin()
                    emit_A0(g + 1)
                    capA = P.capture_end()
                elif l + 1 < depth:
                    P.capture_begin()
                    save = list(ceng)
                    ceng[:] = ["dve"]
                    P.op("sp", lambda e: e.dma_start(out=gains[:], in_=gains_ds[l + 1]), writes=["gains"], dma=True)
                    load_w(wfm_ds[l + 1], Wfm, "Wfm", NFM, 0)
                    load_w(wtm_ds[l + 1], Wtm, "Wtm", NTM, 0)
                    ceng[:] = save
                    capA = P.capture_end()
                P.merge(capB, capA)
                P.capture_begin()
                yk = [f"yT.r{c}" for c in range(4)] + [f"yT.s{c}" for c in range(4)] + ["yT.m0", "yT.m1", "yT.b0", "yT.b1"]
                for c in range(4):
                    cs = slice(c * 128, (c + 1) * 128)
                    for half in range(2):
                        for fb in range(8):
                            P.op("pe", lambda e, fb=fb, half=half, cs=cs: e.matmul(ps[4 + half][:], lhsT=yT[:, fb, cs],
                                                                                    rhs=Wout[:, fb, half * 512:(half + 1) * 512],
                                                                                    start=(fb == 0), stop=(fb == 7)),
                                 reads=yk + ["Wout"], writes=pk(4 + half))
                    ot, otk = ((ox, "ox"), (xt, "xt"))[c % 2]
                    P.op("act", lambda e: e.activation(out=ot[:, 0:512], in_=ps[4][:], func=AF.Copy), reads=pk(4), writes=[otk])
                    P.op("dve", lambda e: e.tensor_copy(out=ot[:, 512:1024], in_=ps[5][:]), reads=pk(5), writes=[otk])
                    P.op("sp", lambda e, c=c: e.dma_start(out=Pd[l][t0 + c * 128:t0 + (c + 1) * 128, :], in_=ot[:]),
                         reads=[otk], writes=[f"P{l}.{g * 4 + c}"], dma=True)
                capC = P.capture_end()
                capA1 = []
                if g + 1 < NG:
                    P.capture_begin()
                    emit_A1(g + 1)
                    capA1 = P.capture_end()
                P.merge(capC, capA1)
                if g % 2 == 1 or g == NG - 1:
                    r0 = (g - (g % 2)) * 512
                    r1 = (g + 1) * 512
                    cks = list(range(r0 // 128, r1 // 128))
                    P.op("pool", lambda e: e.collective_compute("AllReduce", ALU.add, replica_groups=rgroups,
                                                                ins=[Pd[l][r0:r1, :]], outs=[Yd[l][r0:r1, :]]),
                         reads=[f"P{l}.{c}" for c in cks], writes=[f"Y{l}.{c}" for c in cks], dma="cc")
        lf = depth - 1
        P.op("act", lambda e: e.dma_start(out=rt[:], in_=postg_ds[lf].partition_broadcast(128)), writes=["rt"], dma=True)
        KTK = [f"KT{h}.{g}" for h in range(2) for g in range(NG)]
        VVK = [f"VV{cc}" for cc in range(NCH)]
        nx = max(1, min(3, (2 * S) // 2048))
        ktf = KT[:].rearrange("p h s -> p (h s)")
        vvf = VV[:].rearrange("p h c d -> p (h c d)")
        sets = [None] + [(ktf[:, i * 2048:(i + 1) * 2048].bitcast(F32), KTK, vvf[:, i * 2048:(i + 1) * 2048].bitcast(F32), VVK) for i in range(nx)]
        sets = [None] + [(sx[0], [f"fk{i}"], sx[2], [f"fv{i}"]) for i, sx in enumerate(sets[1:])]
        first_use = set()
        for ck in range(NCH):
            rsl = slice(ck * 128, (ck + 1) * 128)
            st = sets[ck % len(sets)]
            if st is None:
                combine_rows(lf, rsl, ck)
                P.op("sp", lambda e: e.dma_start(out=out_d[rsl, :], in_=xt[:]), reads=["xt"], writes=[f"out{ck}", "xt"], dma=True)
            else:
                if ck % len(sets) not in first_use:
                    first_use.add(ck % len(sets))
                    st = (st[0], st[1] + KTK, st[2], st[3] + VVK)
                combine_rows(lf, rsl, ck, bufs=st)
                P.op("sp", lambda e: e.dma_start(out=out_d[rsl, :], in_=st[0]), reads=st[1], writes=[f"out{ck}"] + st[1], dma=True)
        P.op("sp", lambda e: None, reads=[f"out{ck}" for ck in range(NCH)])
        P.emit(sems, dsems)
        stats = P.stats
    return nc, stats


def _consts():
    p = np.arange(128)[:, None]
    f = np.arange(128)[None, :]
    ident = (p == f)
    LE = (p <= f)
    GE = (p >= f)
    LT = (p < f)
    GT = (p > f)
    u = np.arange(896)[None, :] - 384
    neg = np.where(p < u, 0.0, -30000.0)
    ones = np.ones((128, 128))
    return np.concatenate([ident, LE, GE, LT, GT, neg, ones], axis=1).astype(np.float32)


def _rope_tables(S, hh):
    nch = S // 128
    d = 128
    inv_freq = (10000.0 ** (-np.arange(0, d, 2, dtype=np.float32) / np.float32(d))).astype(np.float32)
    pos = np.arange(S, dtype=np.float32)
    ang = (pos[:, None] * inv_freq[None, :]).astype(np.float32)
    cos = np.cos(ang).astype(np.float32).reshape(nch, 128, 64)
    sin = np.sin(ang).astype(np.float32).reshape(nch, 128, 64)
    cc = np.concatenate([cos, cos], -1)
    ss = np.concatenate([-sin, sin], -1)
    n = np.arange(128, dtype=np.float64)
    tab = np.zeros((nch, 128, 2, 4, 128), np.float32)
    for h in range(2):
        gam = 1.0 - 2.0 ** (-5.0 - (2 * hh + h))
        sq = gam ** (n + 1.0)
        sk = (128.0 ** -0.5) * gam ** (-(n + 1.0))
        tab[:, :, 0, h, :] = cc * sq[None, :, None]
        tab[:, :, 1, h, :] = ss * sq[None, :, None]
        tab[:, :, 0, 2 + h, :] = cc * sk[None, :, None]
        tab[:, :, 1, 2 + h, :] = ss * sk[None, :, None]
    return tab.reshape(nch, 128, 1024)


def _layer_inputs(l, hh, w_in, conv_w, conv_b, dt_bias, a_log, d_skip, ssd_norm_g, pre_norm_g, mem_norm_g, w_mem_kv, w_out):
    o = 256 * hh
    W = w_in[l]
    fm_cols = np.r_[3592 + o:3592 + o + 256, 4104 + o:4104 + o + 256, 5128 + o:5128 + o + 256, 5640 + o:5640 + o + 256,
                    6152 + o:6152 + o + 256, 2048 + o:2048 + o + 256, 2560 + 128 * hh:2560 + 128 * hh + 128,
                    2816 + 128 * hh:2816 + 128 * hh + 128]
    tm_cols = np.r_[0 + o:o + 256, 512 + o:512 + o + 256, 1024 + o:1024 + o + 256, 1536 + o:1536 + o + 256,
                    4616 + o:4616 + o + 256, 3072 + o:3072 + o + 256, 3584 + 4 * hh:3584 + 4 * hh + 4]
    ch = np.r_[o:o + 256, 512 + 128 * hh:512 + 128 * hh + 128, 768 + 128 * hh:768 + 128 * hh + 128]
    convp = np.zeros((128, 20), np.float32)
    cw = conv_w[l][:, ch]
    cbias = conv_b[l][ch]
    for cb in range(4):
        for i in range(4):
            convp[:, cb * 4 + i] = cw[i, cb * 128:(cb + 1) * 128]
        convp[:, 16 + cb] = cbias[cb * 128:(cb + 1) * 128]
    rowv = np.zeros((1, 528), np.float32)
    rowv[0, 0:4] = dt_bias[l][4 * hh:4 * hh + 4]
    rowv[0, 4:8] = a_log[l][4 * hh:4 * hh + 4]
    for h in range(2):
        rowv[0, 8 + h] = np.float32((1.0 - 2.0 ** (-5.0 - (2 * hh + h))) ** 128)
    rowv[0, 16:272] = np.repeat(d_skip[l][4 * hh:4 * hh + 4], 64)
    rowv[0, 272:528] = ssd_norm_g[l][o:o + 256]
    gains = np.zeros((128, 16), np.float32)
    gains[:, 0:8] = pre_norm_g[l].reshape(8, 128).T
    gains[:, 8:16] = mem_norm_g[l].reshape(8, 128).T
    wm = w_mem_kv[l]
    wmem = np.concatenate([wm[:, o:o + 256], wm[:, 512 + o:512 + o + 256]], axis=1)
    rows = np.r_[o:o + 256, 512 + o:512 + o + 256, 1024 + o:1024 + o + 256, 1536 + o:1536 + o + 256]
    return dict(wfm=np.ascontiguousarray(W[:, fm_cols]), wtm=np.ascontiguousarray(W[:, tm_cols]),
                wout=np.ascontiguousarray(w_out[l][rows, :]), wmem=np.ascontiguousarray(wmem),
                gains=gains, convp=convp, rowv=rowv)


def kernel(x, mem, pre_norm_g, post_norm_g, w_in, conv_w, conv_b, dt_bias, a_log, d_skip,
           ssd_norm_g, mem_norm_g, w_mem_kv, w_out):
    x = np.asarray(x, np.float32)
    mem = np.asarray(mem, np.float32)
    args = [np.asarray(a, np.float32) for a in (w_in, conv_w, conv_b, dt_bias, a_log, d_skip, ssd_norm_g, pre_norm_g, mem_norm_g, w_mem_kv, w_out)]
    post_norm_g = np.asarray(post_norm_g, np.float32)
    B, S, _ = x.shape
    depth = args[0].shape[0]
    assert B * 2 == 8
    in_maps = make_in_maps(x, mem, args, post_norm_g, list(range(8)))
    nc, _ = build_fused(S, depth)
    res = run_bass_kernel_spmd(nc, in_maps, core_ids=list(range(8)))
    return np.stack([res.results[2 * b]["out"] for b in range(B)], axis=0).astype(np.float32)


def make_in_maps(x, mem, args, post_norm_g, cores):
    S = x.shape[1]
    depth = args[0].shape[0]
    cf = _consts()
    rtabs = [_rope_tables(S, hh) for hh in range(2)]
    in_maps = []
    for core in cores:
        b, hh = core // 2, core % 2
        m = dict(x=np.ascontiguousarray(x[b]), memx=np.ascontiguousarray(mem[b]), cf=cf, rtab=rtabs[hh])
        for l in range(depth):
            for k, v in _layer_inputs(l, hh, *args).items():
                m[f"{k}{l}"] = v
            m[f"postg{l}"] = np.ascontiguousarray(post_norm_g[l].reshape(1, D))
        in_maps.append(m)
    return in_maps
```

```python
import math
from contextlib import ExitStack
import numpy as np
import concourse.bass as bass
import concourse.mybir as mybir
from concourse.bass_utils import run_bass_kernel_spmd

F32 = mybir.dt.float32
BF16 = mybir.dt.bfloat16
AF = mybir.ActivationFunctionType
ALU = mybir.AluOpType
AX = mybir.AxisListType

SAME_ENGINE_SYNC = True
RAW_ONLY_SAME_ENGINE = False
N_DMA_SEMS = 6
DMA_QUEUES = ("sp", "act")
D = 1024
NFM = 1792
NTM = 1540
EPS = 1e-6


class _Op:
    __slots__ = ("eng", "fn", "deps", "is_dma", "signal", "tok", "idx")


class _Rec:
    def __init__(self):
        self.call = None

    def __getattr__(self, name):
        def f(*a, **k):
            self.call = (name, a, k)
            return None
        return f


class Prog:
    ENGS = ("pe", "act", "dve", "pool", "sp")

    def __init__(self, nc):
        self.nc = nc
        self.ops = []
        self.last_w = {}
        self.readers = {}

    ALIAS = {"H2a": ["H2.0", "H2.1"], "H2b": ["H2.2", "H2.3"], "H3c": ["H3.2", "H3.3"], "F2c": ["F2.2"],
             "H6a": ["H6.0", "H6.1"], "H6b": ["H6.2", "H6.3"], "F1z": ["F1.2", "F1.3"]}

    @classmethod
    def xk(cls, keys):
        out = []
        for k in keys:
            if k in cls.ALIAS:
                out += cls.ALIAS[k]
            elif len(k) == 2 and k[0] in "FH" and k[1].isdigit():
                out += [f"{k}.{q}" for q in range(4)]
            else:
                out.append(k)
        return out

    def capture_begin(self):
        self._cap = []

    def capture_end(self):
        c = self._cap
        self._cap = None
        return c

    @staticmethod
    def interleave(a, b):
        out = []
        na, nb = len(a), len(b)
        ia = ib = 0
        while ia < na or ib < nb:
            if ib >= nb or (ia < na and ia * nb <= ib * na):
                out.append(a[ia]); ia += 1
            else:
                out.append(b[ib]); ib += 1
        return out

    def merge(self, a, b):
        na, nb = len(a), len(b)
        ia = ib = 0
        while ia < na or ib < nb:
            if ib >= nb or (ia < na and ia * nb <= ib * na):
                self._reg(*a[ia])
                ia += 1
            else:
                self._reg(*b[ib])
                ib += 1

    def op(self, eng, fn, reads=(), writes=(), dma=False):
        rec = _Rec()
        fn(rec)
        if getattr(self, "_cap", None) is not None:
            self._cap.append((eng, rec.call, reads, writes, dma))
            return None
        return self._reg(eng, rec.call, reads, writes, dma)

    def _reg(self, eng, call, reads=(), writes=(), dma=False):
        reads = self.xk(reads)
        writes = self.xk(writes)
        pr = [k for k in reads if k == "tb" or (k[0] == "p" and k[1:].isdigit())]
        if pr:
            reads = [k for k in reads if k not in pr]
            writes = list(writes) + pr
        o = _Op()
        o.eng, o.fn, o.is_dma, o.signal, o.tok = eng, call, dma, False, None
        o.idx = len(self.ops)
        deps = set()
        raw = set()
        for k in reads:
            w = self.last_w.get(k)
            if w is not None:
                deps.add(w)
                raw.add(w)
        for k in writes:
            w = self.last_w.get(k)
            if w is not None:
                deps.add(w)
            for r in self.readers.get(k, ()):
                deps.add(r)
        if RAW_ONLY_SAME_ENGINE and not dma:
            deps = {d for d in deps if d in raw or self.ops[d].eng != eng or self.ops[d].is_dma}
        for k in reads:
            self.readers.setdefault(k, []).append(o.idx)
        for k in writes:
            self.last_w[k] = o.idx
            self.readers[k] = []
        deps.discard(o.idx)
        o.deps = deps
        self.ops.append(o)
        return o.idx

    def emit(self, sems, dma_sems):
        ops = self.ops
        for o in ops:
            for d in o.deps:
                p = ops[d]
                if p.is_dma:
                    continue
                if p.eng == o.eng and (not o.is_dma) and (p.eng == "pe" or not SAME_ENGINE_SYNC):
                    continue
                p.signal = True
        cnt = {e: 0 for e in self.ENGS}
        dcnt = {e: [0] * N_DMA_SEMS for e in self.ENGS}
        drr = {e: 0 for e in self.ENGS}
        dma_prev = {}
        ccn = [0]
        per_eng = {e: [] for e in self.ENGS}
        for o in ops:
            pre = None
            if o.is_dma == "cc":
                ccn[0] += 1
                o.tok = (dma_sems["cc"], ccn[0], None)
            elif o.is_dma:
                s = drr[o.eng] % N_DMA_SEMS
                drr[o.eng] += 1
                pre = dma_prev.get((o.eng, s))
                dcnt[o.eng][s] += 16
                o.tok = (dma_sems[o.eng][s], dcnt[o.eng][s], 16)
                dma_prev[(o.eng, s)] = o.tok
            elif o.signal:
                cnt[o.eng] += 1
                o.tok = (sems[o.eng], cnt[o.eng], 1)
            per_eng[o.eng].append((o, pre))
        waited = {e: {} for e in self.ENGS}
        nwaits = 0
        plans = {e: [] for e in self.ENGS}
        for e in self.ENGS:
            for o, pre in per_eng[e]:
                toks = []
                if pre is not None:
                    toks.append(pre)
                for d in o.deps:
                    p = ops[d]
                    if p.tok is None:
                        continue
                    if (not p.is_dma) and (not o.is_dma) and p.eng == e and (e == "pe" or not SAME_ENGINE_SYNC):
                        continue
                    toks.append(p.tok)
                best = {}
                for (sem, val, _) in toks:
                    k = id(sem)
                    if val > waited[e].get(k, 0):
                        if k not in best or best[k][1] < val:
                            best[k] = (sem, val)
                ws = []
                for k, (sem, val) in best.items():
                    waited[e][k] = val
                    ws.append((sem, val))
                    nwaits += 1
                plans[e].append((o, ws))
        self.stats = dict(nops=len(ops), nwaits=nwaits, per_eng={e: len(per_eng[e]) for e in self.ENGS})
        nc = self.nc
        with nc.allow_low_precision("bf16 softmax-normaliser tile in the memory-read epilogue"), nc.Block() as block:
            def run(engname, eng):
                for o, ws in plans[engname]:
                    for sem, val in ws:
                        eng.wait_ge(sem, val)
                    inst = None
                    if o.fn is not None:
                        inst = getattr(eng, o.fn[0])(*o.fn[1], **o.fn[2])
                    if inst is not None and o.tok is not None:
                        if o.tok[2] is None:
                            inst.then_inc(o.tok[0])
                        else:
                            inst.then_inc(o.tok[0], o.tok[2])

            @block.tensor
            def _(eng):
                run("pe", eng)

            @block.scalar
            def _(eng):
                run("act", eng)

            @block.vector
            def _(eng):
                run("dve", eng)

            @block.gpsimd
            def _(eng):
                run("pool", eng)

            @block.sync
            def _(eng):
                run("sp", eng)


def build_fused(S, depth=2, rgroups=None):
    NG = S // 512
    NCH = S // 128
    nc = bass.Bass("TRN2", target_bir_lowering=False)

    def din(name, shape):
        return nc.dram_tensor(name, shape, F32, kind="ExternalInput").ap()

    x_d = din("x", [S, D])
    mem_d = din("memx", [256, D])
    if rgroups is None:
        rgroups = [[0, 1], [2, 3], [4, 5], [6, 7]]
    wfm_ds = [din(f"wfm{l}", [D, NFM]) for l in range(depth)]
    wtm_ds = [din(f"wtm{l}", [D, NTM]) for l in range(depth)]
    wout_ds = [din(f"wout{l}", [D, D]) for l in range(depth)]
    wmem_ds = [din(f"wmem{l}", [D, 512]) for l in range(depth)]
    gains_ds = [din(f"gains{l}", [128, 16]) for l in range(depth)]
    convp_ds = [din(f"convp{l}", [128, 20]) for l in range(depth)]
    rowv_ds = [din(f"rowv{l}", [1, 528]) for l in range(depth)]
    postg_ds = [din(f"postg{l}", [1, D]) for l in range(depth)]
    Pd = [nc.dram_tensor(f"Pd{l}", [S, D], F32, kind="Internal", addr_space="Local").ap() for l in range(depth)]
    Yd = [nc.dram_tensor(f"Yd{l}", [S, D], F32, kind="Internal", addr_space="Local").ap() for l in range(depth)]
    Xd = [x_d] + [nc.dram_tensor(f"Xd{l}", [S, D], F32, kind="Internal", addr_space="Local").ap() for l in range(1, depth)]
    cf_d = din("cf", [128, 640 + 896 + 128])
    rtab_d = din("rtab", [NCH, 128, 1024])
    out_d = nc.dram_tensor("out", [S, D], F32, kind="ExternalOutput").ap()

    es = ExitStack()
    with es:
        def sb(name, shape, dt=F32):
            return es.enter_context(nc.sbuf_tensor(name, shape, dt))

        def psum(name, shape, dt=F32):
            return es.enter_context(nc.psum_tensor(name, shape, dt))

        Wfm = sb("Wfm", [128, 8, NFM], BF16)
        Wtm = sb("Wtm", [128, 8, NTM], BF16)
        Wout = sb("Wout", [128, 8, D], BF16)
        KT = sb("KT", [128, 2, S], BF16)
        VV = sb("VV", [128, 2, NCH, 128], BF16)
        hnT = sb("hnT", [128, 8, 512], BF16)
        memT = hnT
        yT = sb("yT", [128, 8, 512], BF16)
        Wmem = yT
        qT = sb("qT", [128, 2, 512], BF16)
        gsb = sb("gsb", [128, 2, 512], BF16)
        qmT = sb("qmT", [128, 2, 512], BF16)
        gm = sb("gm", [128, 2, 512], BF16)
        xcT = sb("xcT", [128, 4, 512], BF16)
        Fs = [sb(f"F{i}", [128, 520], F32) for i in range(5)]
        Hs = [sb(f"H{i}", [128, 512], BF16) for i in range(8)]
        xt = sb("xt", [128, D], F32)
        ox = sb("ox", [128, D], F32)
        hn = sb("hn", [128, D], BF16)
        rt = sb("rt", [128, 1024], F32)
        gains = sb("gains_s", [128, 16], F32)
        convp = sb("convp_s", [128, 20], F32)
        rowv = sb("rowv_s", [128, 528], F32)
        LEf = sb("LEf", [128, 128], F32)
        GTf = sb("GTf", [128, 128], F32)
        onesf = sb("onesf", [128, 128], F32)
        identb = sb("identb", [128, 128], BF16)
        GEb = sb("GEb", [128, 128], BF16)
        LTb = sb("LTb", [128, 128], BF16)
        onesb = sb("onesb", [128, 128], BF16)
        negm = sb("negm", [128, 896], BF16)
        kmT = sb("kmT", [128, 2, 256], BF16)
        vm = sb("vm", [128, 2, 256], BF16)
        Rf = sb("Rf", [128, 2, 128], F32)
        Rb = sb("Rb", [128, 2, 128], BF16)
        Sf = sb("Sf", [128, 256], F32)
        Sb = sb("Sb", [128, 256], BF16)
        halo = sb("halo", [128, 4, 3], F32)
        sm = sb("sm", [128, 288], F32)
        negA = sb("negA", [128, 4], F32)
        hbias = sb("hbias", [128, 4], F32)
        convh = sb("convh", [128, 16], F32)
        ps = [psum(f"ps{i}", [128, 512], F32) for i in range(7)]
        tb = psum("tb", [128, 8, 128], BF16)

        sems = {e: es.enter_context(nc.semaphore("s_" + e)) for e in Prog.ENGS}
        dsems = {e: [es.enter_context(nc.semaphore(f"d_{e}{i}")) for i in range(N_DMA_SEMS)] for e in Prog.ENGS}
        dsems["cc"] = es.enter_context(nc.semaphore("cc_sem"))
        P = Prog(nc)

        def pk(b, s0=0, n=4):
            return [f"p{b}"]

        def tbk(s0=0, n=8):
            return ["tb"]

        rr = [0]

        YKALL = [f"yT.r{c}" for c in range(4)] + [f"yT.s{c}" for c in range(4)] + ["yT.m0", "yT.m1", "yT.b0", "yT.b1"]
        HKALL = [f"hnT{c}" for c in range(4)]

        def dmaq():
            rr[0] += 1
            return DMA_QUEUES[rr[0] % len(DMA_QUEUES)]

        P.op("sp", lambda e: e.dma_start(out=xt[:, 0:640], in_=cf_d[:, 0:640]), writes=["xt"], dma=True)
        P.op("act", lambda e: e.dma_start(out=ox[:, 0:1024], in_=cf_d[:, 640:1664]), writes=["ox"], dma=True)
        P.op("dve", lambda e: e.tensor_copy(out=identb[:], in_=xt[:, 0:128]), reads=["xt"], writes=["identb"])
        P.op("dve", lambda e: e.tensor_copy(out=LEf[:], in_=xt[:, 128:256]), reads=["xt"], writes=["LEf"])
        P.op("dve", lambda e: e.tensor_copy(out=GEb[:], in_=xt[:, 256:384]), reads=["xt"], writes=["GEb"])
        P.op("dve", lambda e: e.tensor_copy(out=LTb[:], in_=xt[:, 384:512]), reads=["xt"], writes=["LTb"])
        P.op("dve", lambda e: e.tensor_copy(out=GTf[:], in_=xt[:, 512:640]), reads=["xt"], writes=["GTf"])
        P.op("dve", lambda e: e.tensor_copy(out=negm[:], in_=ox[:, 0:896]), reads=["ox"], writes=["negm"])
        P.op("dve", lambda e: e.tensor_copy(out=onesf[:], in_=ox[:, 896:1024]), reads=["ox"], writes=["onesf"])
        P.op("dve", lambda e: e.tensor_copy(out=onesb[:], in_=ox[:, 896:1024]), reads=["ox"], writes=["onesb"])
        def combine_rows(lp, rsl, ck, bufs=None):
            if bufs is None:
                bx, kx_, by, ky_ = xt, ["xt"], ox, ["ox"]
            else:
                bx, kx_, by, ky_ = bufs
            P.op("sp", lambda e: e.dma_start(out=bx, in_=Xd[lp][rsl, :]) if bufs else e.dma_start(out=bx[:], in_=Xd[lp][rsl, :]),
                 reads=([f"X{lp}.{ck}"] if lp > 0 else []), writes=kx_, dma=True)
            P.op("act", lambda e: e.dma_start(out=by, in_=Yd[lp][rsl, :]) if bufs else e.dma_start(out=by[:], in_=Yd[lp][rsl, :]),
                 reads=[f"Y{lp}.{ck}"], writes=ky_, dma=True)
            bxa = bx if bufs else bx[:]
            bya = by if bufs else by[:]
            P.op("act", lambda e: e.activation(out=hn[:], in_=bya, func=AF.Square, scale=1.0 / 32.0, accum_out=sm[:, 68:69]),
                 reads=ky_, writes=["hn", "sm68"])
            P.op("act", lambda e: e.activation(out=sm[:, 69:70], in_=sm[:, 68:69], func=AF.Ln, bias=EPS), reads=["sm68"], writes=["sm69"])
            P.op("act", lambda e: e.activation(out=sm[:, 70:71], in_=sm[:, 69:70], func=AF.Exp, scale=-0.5), reads=["sm69"], writes=["sm70"])
            P.op("dve", lambda e: e.scalar_tensor_tensor(out=bya, in0=bya, scalar=sm[:, 70:71], in1=rt[:], op0=ALU.mult, op1=ALU.mult),
                 reads=ky_ + ["sm70", "rt"], writes=ky_)
            P.op("pool", lambda e: e.tensor_tensor(out=bxa, in0=bxa, in1=bya, op=ALU.add), reads=kx_ + ky_, writes=kx_)

        for l in range(depth):
            if l == 0:
                P.op("sp", lambda e: e.dma_start(out=gains[:], in_=gains_ds[l]), writes=["gains"], dma=True)
            P.op("sp", lambda e: e.dma_start(out=convp[:], in_=convp_ds[l]), writes=["convp"], dma=True)
            P.op("sp", lambda e: e.dma_start(out=rowv[:], in_=rowv_ds[l].partition_broadcast(128)), writes=["rowv"], dma=True)
            P.op("dve", lambda e: e.tensor_scalar(out=hbias[:], in0=convp[:, 16:20], scalar1=0.5, scalar2=None, op0=ALU.mult),
                 reads=["convp"], writes=["hbias"])
            P.op("dve", lambda e: e.tensor_scalar(out=convh[:], in0=convp[:, 0:16], scalar1=0.5, scalar2=None, op0=ALU.mult),
                 reads=["convp"], writes=["convh"])
            P.op("act", lambda e: e.activation(out=negA[:], in_=rowv[:, 4:8], func=AF.Exp), reads=["rowv"], writes=["negA"])
            P.op("dve", lambda e: e.tensor_scalar(out=negA[:], in0=negA[:], scalar1=-1.0, scalar2=None, op0=ALU.mult),
                 reads=["negA"], writes=["negA"])
            P.op("pool", lambda e: e.memset(Rf[:], 0.0), writes=["Rf"])
            P.op("pool", lambda e: e.memset(Rb[:], 0.0), writes=["Rb"])
            P.op("pool", lambda e: e.memset(Sf[:], 0.0), writes=["Sf"])
            P.op("pool", lambda e: e.memset(Sb[:], 0.0), writes=["Sb"])
            P.op("pool", lambda e: e.memset(halo[:], 0.0), writes=["halo0", "halo1", "halo2", "halo3"])

            stg = [(xt, "xt"), (ox, "ox"), (rt, "rt")]
            si = [0]
            ceng = ["dve", "act"]

            def load_w(wd, Wt, wkey, ncols, gcol):
                for k in range(8):
                    c0 = 0
                    while c0 < ncols:
                        cw = min(1024, ncols - c0)
                        st, sk = stg[si[0] % 3]
                        ce = ceng[si[0] % len(ceng)]
                        si[0] += 1
                        q = dmaq()
                        P.op(q, lambda e, st=st, k=k, c0=c0, cw=cw: e.dma_start(out=st[:, 0:cw], in_=wd[k * 128:(k + 1) * 128, c0:c0 + cw]),
                             writes=[sk], dma=True)
                        if gcol is None:
                            if ce == "act":
                                f = lambda e, st=st, k=k, c0=c0, cw=cw: e.activation(out=Wt[:, k, c0:c0 + cw], in_=st[:, 0:cw], func=AF.Copy)
                            else:
                                f = lambda e, st=st, k=k, c0=c0, cw=cw: e.tensor_copy(out=Wt[:, k, c0:c0 + cw], in_=st[:, 0:cw])
                        else:
                            if ce == "act":
                                f = lambda e, st=st, k=k, c0=c0, cw=cw: e.activation(out=Wt[:, k, c0:c0 + cw], in_=st[:, 0:cw], func=AF.Copy,
                                                                                      scale=gains[:, gcol + k:gcol + k + 1])
                            else:
                                f = lambda e, st=st, k=k, c0=c0, cw=cw: e.tensor_scalar(out=Wt[:, k, c0:c0 + cw], in0=st[:, 0:cw],
                                                                                        scalar1=gains[:, gcol + k:gcol + k + 1], scalar2=None, op0=ALU.mult)
                        P.op(ce, f, reads=[sk, "gains"], writes=(YKALL if wkey == "Wmem" else [wkey]))
                        c0 += cw

            load_w(wmem_ds[l], Wmem, "Wmem", 512, 8)
            if l == 0:
                load_w(wfm_ds[l], Wfm, "Wfm", NFM, 0)
                load_w(wtm_ds[l], Wtm, "Wtm", NTM, 0)
            load_w(wout_ds[l], Wout, "Wout", D, None)

            def norm_rows(src_ap):
                if src_ap is not None:
                    P.op("sp", lambda e: e.dma_start(out=xt[:], in_=src_ap), writes=["xt"], dma=True)
                P.op("act", lambda e: e.activation(out=hn[:], in_=xt[:], func=AF.Square, scale=1.0 / 32.0, accum_out=sm[:, 0:1]),
                     reads=["xt"], writes=["hn", "sm0"])
                P.op("act", lambda e: e.activation(out=sm[:, 1:2], in_=sm[:, 0:1], func=AF.Ln, bias=EPS), reads=["sm0"], writes=["sm1"])
                P.op("act", lambda e: e.activation(out=sm[:, 2:3], in_=sm[:, 1:2], func=AF.Exp, scale=-0.5), reads=["sm1"], writes=["sm2"])
                P.op("dve", lambda e: e.tensor_scalar(out=hn[:], in0=xt[:], scalar1=sm[:, 2:3], scalar2=None, op0=ALU.mult),
                     reads=["xt", "sm2"], writes=["hn"])

            for mc in range(2):
                norm_rows(mem_d[mc * 128:(mc + 1) * 128, :])
                for k in range(8):
                    P.op("pe", lambda e, k=k: e.transpose(out=tb[:, k, :], in_=hn[:, k * 128:(k + 1) * 128], identity=identb[:]),
                         reads=["hn", "identb"], writes=["tb"])
                P.op("dve", lambda e, mc=mc: e.tensor_copy(out=memT[:, :, mc * 128:(mc + 1) * 128], in_=tb[:]), reads=tbk(), writes=HKALL)
            for h in range(2):
                for k in range(8):
                    P.op("pe", lambda e, h=h, k=k: e.matmul(ps[0][:, 0:256], lhsT=Wmem[:, k, h * 128:(h + 1) * 128], rhs=memT[:, k, 0:256],
                                                             start=(k == 0), stop=(k == 7)), reads=YKALL + HKALL, writes=pk(0, 0, 2))
                P.op("dve", lambda e, h=h: e.tensor_copy(out=kmT[:, h, :], in_=ps[0][:, 0:256]), reads=pk(0, 0, 2), writes=["kmT"])
            for mb in range(2):
                for k in range(8):
                    P.op("pe", lambda e, mb=mb, k=k: e.matmul(ps[1][:, 0:256], lhsT=memT[:, k, mb * 128:(mb + 1) * 128], rhs=Wmem[:, k, 256:512],
                                                               start=(k == 0), stop=(k == 7)), reads=YKALL + HKALL, writes=pk(1, 0, 2))
                P.op("dve", lambda e, mb=mb: e.tensor_copy(out=vm[:, mb, :], in_=ps[1][:, 0:256]), reads=pk(1, 0, 2), writes=["vm"])

            F0, F1, F2, F3, F4 = Fs
            H0, H1, H2, H3, H4, H5, H6, H7 = Hs

            g128 = [rowv[:, 8:9], rowv[:, 9:10]]
            dsk = rowv[:, 16:272]
            sng = rowv[:, 272:528]

            for g in range(NG):
                t0 = g * 512
                def emit_A0(g):
                    t0 = g * 512
                    if l > 0:
                        P.op("act", lambda e: e.dma_start(out=rt[:], in_=postg_ds[l - 1].partition_broadcast(128)), writes=["rt"], dma=True)
                    for c in range(4):
                        rsl = slice(t0 + c * 128, t0 + (c + 1) * 128)
                        if l == 0:
                            norm_rows(x_d[rsl, :])
                        else:
                            combine_rows(l - 1, rsl, g * 4 + c)
                            P.op("sp", lambda e: e.dma_start(out=Xd[l][rsl, :], in_=xt[:]), reads=["xt"], writes=[f"X{l}.{g * 4 + c}"], dma=True)
                            norm_rows(None)
                        for k in range(8):
                            P.op("pe", lambda e, k=k: e.transpose(out=tb[:, k, :], in_=hn[:, k * 128:(k + 1) * 128], identity=identb[:]),
                                 reads=["hn", "identb"], writes=["tb"])
                        P.op("dve", lambda e, c=c: e.tensor_copy(out=hnT[:, :, c * 128:(c + 1) * 128], in_=tb[:]), reads=tbk(), writes=[f"hnT{c}"])
                if g == 0:
                    emit_A0(0)
                def emit_A1(g):
                    t0 = g * 512
                    hnTk = [f"hnT{c}" for c in range(4)]
                    def fm_block(j, bank):
                        for k in range(8):
                            P.op("pe", lambda e, k=k: e.matmul(ps[bank][:], lhsT=Wfm[:, k, j * 128:(j + 1) * 128], rhs=hnT[:, k, :],
                                                               start=(k == 0), stop=(k == 7)), reads=["Wfm"] + hnTk, writes=pk(bank))

                    sc = 1.0 / math.sqrt(128.0)
                    for jj, j in enumerate([10, 11, 12, 13, 0, 1, 2, 3, 4, 5, 6, 7, 8, 9]):
                        bank = jj % 4
                        fm_block(j, bank)
                        src = ps[bank]
                        rk = pk(bank)
                        if j < 2:
                            P.op("dve", lambda e, j=j, src=src: e.tensor_scalar(out=qT[:, j, :], in0=src[:], scalar1=sc, scalar2=None, op0=ALU.mult),
                                 reads=rk, writes=[f"qT{j}"])
                        elif j < 4:
                            h = j - 2
                            P.op("dve", lambda e, h=h, src=src: e.tensor_copy(out=KT[:, h, t0:t0 + 512], in_=src[:]), reads=rk, writes=[f"KT{h}.{g}"])
                        elif j < 6:
                            h = j - 4
                            P.op("act", lambda e, src=src: e.activation(out=F4[:, 0:512], in_=src[:], func=AF.Tanh, scale=0.5), reads=rk, writes=["F4"])
                            P.op("dve", lambda e, h=h, src=src: e.scalar_tensor_tensor(out=gsb[:, h, :], in0=F4[:, 0:512], scalar=1.0, in1=src[:],
                                                                                        op0=ALU.add, op1=ALU.mult), reads=rk + ["F4"], writes=[f"gsb{h}"])
                        elif j < 8:
                            h = j - 6
                            P.op("dve", lambda e, h=h, src=src: e.tensor_scalar(out=qmT[:, h, :], in0=src[:], scalar1=sc, scalar2=None, op0=ALU.mult),
                                 reads=rk, writes=[f"qmT{h}"])
                        elif j < 10:
                            h = j - 8
                            P.op("act", lambda e, src=src: e.activation(out=F4[:, 0:512], in_=src[:], func=AF.Tanh, scale=0.5), reads=rk, writes=["F4"])
                            P.op("dve", lambda e, h=h, src=src: e.scalar_tensor_tensor(out=gm[:, h, :], in0=F4[:, 0:512], scalar=1.0, in1=src[:],
                                                                                        op0=ALU.add, op1=ALU.mult), reads=rk + ["F4"], writes=[f"gm{h}"])
                        else:
                            cb = j - 10
                            Fr, Fa = ((F0, F1), (F2, F3))[cb % 2]
                            kr, ka = (("F0", "F1"), ("F2", "F3"))[cb % 2]
                            P.op("pool", lambda e, cb=cb: e.tensor_copy(out=Fr[:, 0:3], in_=halo[:, cb, :]), reads=[f"halo{cb}"], writes=[kr])
                            P.op("dve", lambda e, src=src: e.tensor_copy(out=Fr[:, 3:515], in_=src[:]), reads=rk, writes=[kr])
                            P.op("pool", lambda e, cb=cb: e.tensor_copy(out=halo[:, cb, :], in_=Fr[:, 512:515]), reads=[kr], writes=[f"halo{cb}"])
                            P.op("pool", lambda e, cb=cb: e.tensor_scalar(out=Fa[:, 0:512], in0=Fr[:, 0:512], scalar1=convh[:, cb * 4:cb * 4 + 1],
                                                                           scalar2=hbias[:, cb:cb + 1], op0=ALU.mult, op1=ALU.add),
                                 reads=[kr, "convh", "hbias"], writes=[ka])
                            for i in range(1, 4):
                                P.op("dve", lambda e, cb=cb, i=i: e.scalar_tensor_tensor(out=Fa[:, 0:512], in0=Fr[:, i:i + 512],
                                                                                           scalar=convh[:, cb * 4 + i:cb * 4 + i + 1], in1=Fa[:, 0:512],
                                                                                           op0=ALU.mult, op1=ALU.add), reads=[kr, ka, "convh"], writes=[ka])
                            P.op("act", lambda e: e.activation(out=Fr[:, 0:512], in_=Fa[:, 0:512], func=AF.Tanh), reads=[ka], writes=[kr])
                            P.op("dve", lambda e, cb=cb: e.scalar_tensor_tensor(out=xcT[:, cb, :], in0=Fr[:, 0:512], scalar=1.0, in1=Fa[:, 0:512],
                                                                                 op0=ALU.add, op1=ALU.mult), reads=[ka, kr], writes=[f"xcT{cb}"])
                if g == 0:
                    emit_A1(0)
                for c in range(4):
                    for k in range(8):
                        P.op("pe", lambda e, k=k, c=c: e.matmul(ps[5][:, c * 4:(c + 1) * 4], lhsT=hnT[:, k, c * 128:(c + 1) * 128], rhs=Wtm[:, k, 1536:1540],
                                                                 start=(k == 0), stop=(k == 7)), reads=["Wtm", f"hnT{c}"], writes=pk(5))
                v44 = lambda ap: ap.rearrange("p (c h) -> p c h", c=4)
                P.op("dve", lambda e: e.tensor_tensor(out=v44(sm[:, 100:116]), in0=v44(ps[5][:, 0:16]),
                                                      in1=rowv[:, 0:4].unsqueeze(1).to_broadcast([128, 4, 4]), op=ALU.add),
                     reads=pk(5) + ["rowv"], writes=["smB0"])
                P.op("act", lambda e: e.activation(out=sm[:, 116:132], in_=sm[:, 100:116], func=AF.Exp), reads=["smB0"], writes=["smB1"])
                P.op("act", lambda e: e.activation(out=sm[:, 132:148], in_=sm[:, 116:132], func=AF.Ln, bias=1.0), reads=["smB1"], writes=["smB"])
                P.op("dve", lambda e: e.tensor_tensor(out=v44(sm[:, 148:164]), in0=v44(sm[:, 132:148]),
                                                      in1=negA[:].unsqueeze(1).to_broadcast([128, 4, 4]), op=ALU.mult),
                     reads=["smB", "negA"], writes=["smB"])
                P.op("pe", lambda e: e.matmul(ps[1][:, 384:400], lhsT=LEf[:], rhs=sm[:, 148:164], start=True, stop=True), reads=["LEf", "smB"], writes=pk(1))
                P.op("pe", lambda e: e.matmul(ps[1][:, 400:416], lhsT=onesf[:], rhs=sm[:, 148:164], start=True, stop=True), reads=["onesf", "smB"], writes=pk(1))
                P.op("dve", lambda e: e.tensor_copy(out=sm[:, 164:196], in_=ps[1][:, 384:416]), reads=pk(1), writes=["smB"])
                P.op("act", lambda e: e.activation(out=sm[:, 196:228], in_=sm[:, 164:196], func=AF.Exp), reads=["smB"], writes=["smB"])
                P.op("dve", lambda e: e.tensor_tensor(out=sm[:, 228:244], in0=sm[:, 180:196], in1=sm[:, 164:180], op=ALU.subtract), reads=["smB"], writes=["smB"])
                P.op("act", lambda e: e.activation(out=sm[:, 244:260], in_=sm[:, 228:244], func=AF.Exp), reads=["smB"], writes=["smB"])
                P.op("dve", lambda e: e.tensor_tensor(out=sm[:, 260:276], in0=sm[:, 244:260], in1=sm[:, 132:148], op=ALU.mult), reads=["smB"], writes=["smB"])
                for c in range(4):
                    cc = g * 4 + c
                    cs = slice(c * 128, (c + 1) * 128)
                    P.op("act", lambda e, cc=cc: e.dma_start(out=rt[:], in_=rtab_d[cc]), writes=["rt"], dma=True)
                    for (bank, c0, cw) in ((2, 0, 512), (3, 512, 512), (4, 1024, 512)):
                        for k in range(8):
                            P.op("pe", lambda e, k=k, bank=bank, c0=c0, cw=cw: e.matmul(ps[bank][:, 0:cw], lhsT=hnT[:, k, cs], rhs=Wtm[:, k, c0:c0 + cw],
                                                                                         start=(k == 0), stop=(k == 7)),
                                 reads=["Wtm", f"hnT{c}"], writes=(pk(bank) if cw == 512 else pk(bank, 0, 1)))
                    P.capture_begin()
                    qk4 = ps[2][:].rearrange("p (w h d) -> p w h d", w=4, h=2)
                    rtS = rt[:, 512:1024].rearrange("p (w h d) -> p w h d", w=4, h=2)
                    F1v = F1[:, 0:512].rearrange("p (w h d) -> p w h d", w=4, h=2)
                    P.op("dve", lambda e: e.tensor_tensor(out=F0[:, 0:512], in0=ps[2][:], in1=rt[:, 0:512], op=ALU.mult),
                         reads=pk(2) + ["rt"], writes=["F0"])
                    P.op("dve", lambda e: e.tensor_tensor(out=F1v[:, :, 0, :], in0=qk4[:, :, 1, :], in1=rtS[:, :, 0, :], op=ALU.mult),
                         reads=pk(2) + ["rt"], writes=["F1"])
                    P.op("dve", lambda e: e.tensor_tensor(out=F1v[:, :, 1, :], in0=qk4[:, :, 0, :], in1=rtS[:, :, 1, :], op=ALU.mult),
                         reads=pk(2) + ["rt"], writes=["F1"])
                    P.op("pool", lambda e: e.tensor_tensor(out=H0[:], in0=F0[:, 0:512], in1=F1[:, 0:512], op=ALU.add),
                         reads=["F0", "F1"], writes=["H0"])
                    P.op("dve", lambda e: e.tensor_copy(out=H2[:, 0:256], in_=ps[3][:, 0:256]), reads=pk(3, 0, 2), writes=["H2a"])
                    P.op("act", lambda e: e.activation(out=F1[:, 0:256], in_=ps[3][:, 256:512], func=AF.Exp, scale=-1.0), reads=pk(3, 2, 2), writes=["F1"])
                    P.op("act", lambda e: e.activation(out=F1[:, 256:512], in_=F1[:, 0:256], func=AF.Ln, bias=1.0), reads=["F1"], writes=["F1"])
                    P.op("act", lambda e: e.activation(out=F1[:, 0:256], in_=F1[:, 256:512], func=AF.Exp, scale=-1.0), reads=["F1"], writes=["F1"])
                    P.op("dve", lambda e: e.tensor_tensor(out=H2[:, 256:512], in0=ps[3][:, 256:512], in1=F1[:, 0:256], op=ALU.mult),
                         reads=pk(3, 2, 2) + ["F1"], writes=["H2b"])
                    P.op("dve", lambda e, cc=cc: e.tensor_copy(out=VV[:, :, cc, :], in_=ps[4][:, 0:256].rearrange("p (h d) -> p h d", h=2)),
                         reads=pk(4, 0, 2), writes=[f"VV{cc}"])
                    for w in range(4):
                        P.op("pe", lambda e, w=w: e.transpose(out=tb[:, w, :], in_=H0[:, w * 128:(w + 1) * 128], identity=identb[:]),
                             reads=["H0", "identb"], writes=["tb"])
                    P.op("dve", lambda e: e.tensor_copy(out=H1[:].rearrange("p (w d) -> p w d", w=4), in_=tb[:, 0:4, :]), reads=tbk(0, 4), writes=["H1"])
                    for h in range(2):
                        P.op("pe", lambda e, h=h: e.matmul(ps[3][:, h * 128:(h + 1) * 128], lhsT=H1[:, (2 + h) * 128:(3 + h) * 128],
                                                            rhs=H1[:, h * 128:(h + 1) * 128], start=True, stop=True),
                             reads=["H1"], writes=pk(3, h, 1))
                        P.op("dve", lambda e, h=h: e.tensor_tensor(out=H3[:, h * 128:(h + 1) * 128], in0=ps[3][:, h * 128:(h + 1) * 128],
                                                                    in1=LEf[:], op=ALU.mult), reads=pk(3, h, 1) + ["LEf"], writes=[f"H3.{h}"])
                        P.op("pe", lambda e, h=h: e.matmul(ps[6][:, h * 128:(h + 1) * 128], lhsT=H3[:, h * 128:(h + 1) * 128],
                                                            rhs=H2[:, h * 128:(h + 1) * 128], start=True, stop=False),
                             reads=[f"H3.{h}", "H2a"], writes=pk(6, h, 1))
                        P.op("pe", lambda e, h=h: e.matmul(ps[6][:, h * 128:(h + 1) * 128], lhsT=H1[:, h * 128:(h + 1) * 128],
                                                            rhs=Rb[:, h, :], start=False, stop=True),
                             reads=["H1", "Rb"], writes=pk(6, h, 1))
                        P.op("pe", lambda e, h=h: e.matmul(ps[6][:, 256 + h * 128:384 + h * 128], lhsT=H0[:, (2 + h) * 128:(3 + h) * 128],
                                                            rhs=H2[:, h * 128:(h + 1) * 128], start=True, stop=True),
                             reads=["H0", "H2a"], writes=pk(6, 2 + h, 1))
                        P.op("dve", lambda e, h=h: e.scalar_tensor_tensor(out=Rf[:, h, :], in0=Rf[:, h, :], scalar=g128[h],
                                                                           in1=ps[6][:, 256 + h * 128:384 + h * 128], op0=ALU.mult, op1=ALU.add),
                             reads=["Rf", "rowv"] + pk(6, 2 + h, 1), writes=["Rf"])
                        P.op("act", lambda e, h=h: e.activation(out=Rb[:, h, :], in_=Rf[:, h, :], func=AF.Copy, scale=g128[h]),
                             reads=["Rf", "rowv"], writes=["Rb"])
                    y3 = ps[6][:, 0:256].rearrange("p (h d) -> p h d", h=2)
                    P.op("dve", lambda e: e.tensor_reduce(out=sm[:, 4:6], in_=y3, axis=AX.X, op=ALU.add), reads=pk(6, 0, 2), writes=["sm4"])
                    for h in range(2):
                        P.op("act", lambda e, h=h: e.activation(out=F2[:, h * 128:(h + 1) * 128], in_=ps[6][:, h * 128:(h + 1) * 128], func=AF.Square,
                                                                 accum_out=sm[:, 6 + h:7 + h]), reads=pk(6, h, 1), writes=["F2", f"sm6{h}"])
                    P.op("dve", lambda e: e.tensor_scalar(out=sm[:, 8:10], in0=sm[:, 4:6], scalar1=1.0 / 128, scalar2=None, op0=ALU.mult),
                         reads=["sm4"], writes=["sm8"])
                    P.op("dve", lambda e: e.tensor_tensor(out=sm[:, 10:12], in0=sm[:, 8:10], in1=sm[:, 8:10], op=ALU.mult), reads=["sm8"], writes=["sm10"])
                    P.op("dve", lambda e: e.scalar_tensor_tensor(out=sm[:, 12:14], in0=sm[:, 6:8], scalar=1.0 / 128, in1=sm[:, 10:12],
                                                                 op0=ALU.mult, op1=ALU.subtract), reads=["sm60", "sm61", "sm10"], writes=["sm12"])
                    P.op("act", lambda e: e.activation(out=sm[:, 14:16], in_=sm[:, 12:14], func=AF.Ln, bias=EPS), reads=["sm12"], writes=["sm14"])
                    P.op("act", lambda e: e.activation(out=sm[:, 16:18], in_=sm[:, 14:16], func=AF.Exp, scale=-0.5), reads=["sm14"], writes=["sm16"])
                    for h in range(2):
                        P.op("dve", lambda e, h=h: e.tensor_scalar(out=F2[:, h * 128:(h + 1) * 128], in0=ps[6][:, h * 128:(h + 1) * 128],
                                                                    scalar1=sm[:, 8 + h:9 + h], scalar2=sm[:, 16 + h:17 + h],
                                                                    op0=ALU.subtract, op1=ALU.mult), reads=pk(6, h, 1) + ["sm8", "sm16"], writes=["F2"])
                    P.op("pool", lambda e: e.tensor_tensor(out=H3[:, 256:512], in0=F2[:, 0:256], in1=H2[:, 256:512], op=ALU.mult),
                         reads=["F2", "H2b"], writes=["H3c"])
                    for h in range(2):
                        P.op("pe", lambda e, h=h: e.transpose(out=tb[:, 4 + h, :], in_=H3[:, 256 + h * 128:384 + h * 128], identity=identb[:]),
                             reads=["H3c", "identb"], writes=["tb"])
                    P.op("dve", lambda e: e.tensor_copy(out=yT[:, 0:2, cs], in_=tb[:, 4:6, :]), reads=tbk(4, 2), writes=[f"yT.r{c}"])
                    cap_ret = P.capture_end()
                    P.capture_begin()
                    o4 = 4 * c
                    for h in range(4):
                        P.op("dve", lambda e, h=h: e.tensor_scalar(out=F3[:, h * 128:(h + 1) * 128], in0=GTf[:], scalar1=sm[:, 148 + o4 + h:149 + o4 + h],
                                                                    scalar2=None, op0=ALU.mult), reads=["GTf", "smB"], writes=[f"F3.{h}"])
                        P.op("pe", lambda e, h=h: e.matmul(ps[0][:, h * 128:(h + 1) * 128], lhsT=F3[:, h * 128:(h + 1) * 128], rhs=LEf[:],
                                                            start=True, stop=True), reads=[f"F3.{h}", "LEf"], writes=pk(0, h, 1))
                        P.op("act", lambda e, h=h: e.activation(out=F4[:, h * 128:(h + 1) * 128], in_=ps[0][:, h * 128:(h + 1) * 128], func=AF.Exp),
                             reads=pk(0, h, 1), writes=[f"F4.{h}"])
                    P.op("pe", lambda e: e.matmul(ps[1][:, 0:128], lhsT=xcT[:, 2, cs], rhs=xcT[:, 3, cs], start=True, stop=True),
                         reads=["xcT2", "xcT3"], writes=pk(1, 0, 1))
                    P.op("dve", lambda e: e.tensor_tensor(out=F2[:, 256:384], in0=ps[1][:, 0:128], in1=LEf[:], op=ALU.mult),
                         reads=pk(1, 0, 1) + ["LEf"], writes=["F2c"])
                    for h in range(4):
                        P.op("pool", lambda e, h=h: e.tensor_tensor(out=H4[:, h * 128:(h + 1) * 128], in0=F2[:, 256:384], in1=F4[:, h * 128:(h + 1) * 128],
                                                                     op=ALU.mult), reads=["F2c", f"F4.{h}"], writes=[f"H4.{h}"])
                    for i in range(3):
                        P.op("pe", lambda e, i=i: e.transpose(out=tb[:, i, :], in_=xcT[:, i, cs], identity=identb[:]),
                             reads=[f"xcT{i}", "identb"], writes=["tb"])
                    P.op("dve", lambda e: e.tensor_copy(out=H5[:, 0:384].rearrange("p (w d) -> p w d", w=3), in_=tb[:, 0:3, :]), reads=tbk(0, 3), writes=["H5"])
                    xs3 = H5[:, 0:256].rearrange("p (h d) -> p h d", h=4)
                    P.op("dve", lambda e: e.tensor_tensor(out=H6[:, 0:256].rearrange("p (h d) -> p h d", h=4), in0=xs3,
                                                          in1=sm[:, 132 + o4:136 + o4].unsqueeze(2).to_broadcast([128, 4, 64]), op=ALU.mult),
                         reads=["H5", "smB"], writes=["H6a"])
                    P.op("dve", lambda e: e.tensor_tensor(out=H6[:, 256:512].rearrange("p (h d) -> p h d", h=4), in0=xs3,
                                                          in1=sm[:, 260 + o4:264 + o4].unsqueeze(2).to_broadcast([128, 4, 64]), op=ALU.mult),
                         reads=["H5", "smB"], writes=["H6b"])
                    for h in range(4):
                        P.op("pe", lambda e, h=h: e.matmul(ps[1][:, 128 + h * 64:192 + h * 64], lhsT=H4[:, h * 128:(h + 1) * 128],
                                                            rhs=H6[:, h * 64:(h + 1) * 64], start=True, stop=True),
                             reads=[f"H4.{h}", "H6a"], writes=pk(1, 1, 2))
                    P.op("pe", lambda e: e.matmul(ps[2][:, 0:256], lhsT=xcT[:, 3, cs], rhs=Sb[:], start=True, stop=True),
                         reads=["xcT3", "Sb"], writes=pk(2, 0, 2))
                    P.op("pe", lambda e: e.matmul(ps[2][:, 256:512], lhsT=H5[:, 256:384], rhs=H6[:, 256:512], start=True, stop=True),
                         reads=["H5", "H6b"], writes=pk(2, 2, 2))
                    P.op("dve", lambda e: e.tensor_tensor(out=F3[:, 0:256].rearrange("p (h d) -> p h d", h=4),
                                                          in0=ps[2][:, 0:256].rearrange("p (h d) -> p h d", h=4),
                                                          in1=sm[:, 196 + o4:200 + o4].unsqueeze(2).to_broadcast([128, 4, 64]), op=ALU.mult),
                         reads=pk(2, 0, 2) + ["smB"], writes=["F3.0", "F3.1"])
                    P.op("dve", lambda e: e.tensor_tensor(out=F3[:, 0:256], in0=F3[:, 0:256], in1=ps[1][:, 128:384], op=ALU.add),
                         reads=["F3.0", "F3.1"] + pk(1, 1, 2), writes=["F3.0", "F3.1"])
                    P.op("pool", lambda e: e.tensor_tensor(out=F3[:, 256:512], in0=H5[:, 0:256], in1=dsk, op=ALU.mult), reads=["H5", "rowv"], writes=["F3.2", "F3.3"])
                    P.op("pool", lambda e: e.tensor_tensor(out=F3[:, 0:256], in0=F3[:, 0:256], in1=F3[:, 256:512], op=ALU.add), reads=["F3.0", "F3.1", "F3.2", "F3.3"], writes=["F3.0", "F3.1"])
                    P.op("dve", lambda e: e.tensor_tensor(out=Sf[:].rearrange("p (h d) -> p h d", h=4), in0=Sf[:].rearrange("p (h d) -> p h d", h=4),
                                                          in1=sm[:, 212 + o4:216 + o4].unsqueeze(2).to_broadcast([128, 4, 64]), op=ALU.mult),
                         reads=["Sf", "smB"], writes=["Sf"])
                    P.op("dve", lambda e: e.tensor_tensor(out=Sf[:], in0=Sf[:], in1=ps[2][:, 256:512], op=ALU.add), reads=["Sf"] + pk(2, 2, 2), writes=["Sf"])
                    P.op("act", lambda e: e.activation(out=Sb[:], in_=Sf[:], func=AF.Copy), reads=["Sf"], writes=["Sb"])
                    P.op("act", lambda e: e.activation(out=F4[:, 256:512], in_=ps[4][:, 256:512], func=AF.Exp, scale=-1.0), reads=pk(4, 2, 2), writes=["F4.2", "F4.3"])
                    P.op("act", lambda e: e.activation(out=F4[:, 0:256], in_=F4[:, 256:512], func=AF.Ln, bias=1.0), reads=["F4.2", "F4.3"], writes=["F4.0", "F4.1"])
                    P.op("act", lambda e: e.activation(out=F4[:, 256:512], in_=F4[:, 0:256], func=AF.Exp, scale=-1.0), reads=["F4.0", "F4.1"], writes=["F4.2", "F4.3"])
                    P.op("dve", lambda e: e.tensor_tensor(out=F4[:, 0:256], in0=ps[4][:, 256:512], in1=F4[:, 256:512], op=ALU.mult),
                         reads=pk(4, 2, 2) + ["F4.2", "F4.3"], writes=["F4.0", "F4.1"])
                    P.op("pool", lambda e: e.tensor_tensor(out=F3[:, 0:256], in0=F3[:, 0:256], in1=F4[:, 0:256], op=ALU.mult), reads=["F3.0", "F3.1", "F4.0", "F4.1"], writes=["F3.0", "F3.1"])
                    P.op("act", lambda e: e.activation(out=F4[:, 0:256], in_=F3[:, 0:256], func=AF.Square, scale=1.0 / 16.0, accum_out=sm[:, 64:65]),
                         reads=["F3.0", "F3.1"], writes=["F4.0", "F4.1", "sm64"])
                    P.op("act", lambda e: e.activation(out=sm[:, 65:66], in_=sm[:, 64:65], func=AF.Ln, bias=EPS), reads=["sm64"], writes=["sm65"])
                    P.op("act", lambda e: e.activation(out=sm[:, 66:67], in_=sm[:, 65:66], func=AF.Exp, scale=-0.5), reads=["sm65"], writes=["sm66"])
                    P.op("dve", lambda e: e.scalar_tensor_tensor(out=H7[:, 0:256], in0=F3[:, 0:256], scalar=sm[:, 66:67], in1=sng,
                                                                 op0=ALU.mult, op1=ALU.mult), reads=["F3.0", "F3.1", "sm66", "rowv"], writes=["H7"])
                    for i in range(2):
                        P.op("pe", lambda e, i=i: e.transpose(out=tb[:, 6 + i, :], in_=H7[:, i * 128:(i + 1) * 128], identity=identb[:]),
                             reads=["H7", "identb"], writes=["tb"])
                    P.op("dve", lambda e: e.tensor_copy(out=yT[:, 2:4, cs], in_=tb[:, 6:8, :]), reads=tbk(6, 2), writes=[f"yT.s{c}"])
                    cap_ssd = P.capture_end()
                    P.merge(cap_ret, cap_ssd)
                P.capture_begin()
                for h in range(2):
                    for mb in range(2):
                        P.op("pe", lambda e, h=h, mb=mb: e.matmul(ps[5 + mb][:], lhsT=kmT[:, h, mb * 128:(mb + 1) * 128], rhs=qmT[:, h, :],
                                                                   start=True, stop=True), reads=["kmT", f"qmT{h}"], writes=pk(5 + mb))
                        P.op("act", lambda e, mb=mb: e.activation(out=Hs[6 + mb][:], in_=ps[5 + mb][:], func=AF.Exp), reads=pk(5 + mb), writes=[f"H{6 + mb}"])
                    for mb in range(2):
                        P.op("pe", lambda e, mb=mb: e.matmul(ps[5][:], lhsT=onesb[:], rhs=Hs[6 + mb][:], start=(mb == 0), stop=(mb == 1)),
                             reads=["onesb", f"H{6 + mb}"], writes=pk(5))
                    for mb in range(2):
                        P.op("pe", lambda e, h=h, mb=mb: e.matmul(ps[6][:], lhsT=vm[:, mb, h * 128:(h + 1) * 128], rhs=Hs[6 + mb][:],
                                                                   start=(mb == 0), stop=(mb == 1)), reads=["vm", f"H{6 + mb}"], writes=pk(6))
                    P.op("dve", lambda e: e.reciprocal(out=H6[:], in_=ps[5][:]), reads=pk(5), writes=["H6"])
                    P.op("dve", lambda e: e.scalar_tensor_tensor(out=H7[:], in0=ps[6][:], scalar=0.5, in1=H6[:], op0=ALU.mult, op1=ALU.mult),
                         reads=pk(6) + ["H6"], writes=["H7"])
                    P.op("pool", lambda e, h=h: e.tensor_tensor(out=yT[:, 6 + h, :], in0=H7[:], in1=gm[:, h, :], op=ALU.mult),
                         reads=["H7", f"gm{h}"], writes=[f"yT.m{h}"])
                capA5 = P.capture_end()
                P.capture_begin()
                for h in range(2):
                    units = [("d", j) for j in (3, 2, 1, 0)] + [("o", kb) for kb in range(4 * g - 1, -1, -1)]
                    n = len(units)
                    E = [F0, F1, F2, F3]
                    Gt = [F4, F4]
                    SP = [H0, H1]
                    Wb = [H2, H3]
                    SS = [H4, H5]
                    CB = [2, 4]

                    def qk(i):
                        kind, v = units[i]
                        kb = 4 * g + v if kind == "d" else v
                        gi = kb // 4
                        bank = i % 2
                        P.op("pe", lambda e: e.matmul(ps[bank][:], lhsT=KT[:, h, kb * 128:(kb + 1) * 128], rhs=qT[:, h, :],
                                                      start=True, stop=(kind != "d")), reads=[f"KT{h}.{gi}", f"qT{h}"], writes=pk(bank))
                        if kind == "d":
                            o0 = 384 - 128 * v
                            P.op("pe", lambda e: e.matmul(ps[bank][:], lhsT=identb[:], rhs=negm[:, o0:o0 + 512], start=False, stop=True),
                                 reads=["identb", "negm"], writes=pk(bank))

                    def e_(i):
                        P.op("act", lambda e: e.activation(out=E[i % 4][:, 0:512], in_=ps[i % 2][:], func=AF.Exp), reads=pk(i % 2), writes=[f"F{i % 4}"])

                    def sp_(i):
                        P.op("act", lambda e: e.activation(out=SP[i % 2][:], in_=E[i % 4][:, 0:512], func=AF.Ln, bias=1.0),
                             reads=[f"F{i % 4}"], writes=[f"H{i % 2}"])

                    def cmm(i):
                        cb = CB[i % 2]
                        P.op("pe", lambda e: e.matmul(ps[cb][:], lhsT=GEb[:], rhs=SP[i % 2][:], start=True, stop=(i == 0)),
                             reads=["GEb", f"H{i % 2}"], writes=pk(cb))
                        if i > 0:
                            P.op("pe", lambda e: e.matmul(ps[cb][:], lhsT=onesb[:], rhs=SS[(i - 1) % 2][:], start=False, stop=True),
                                 reads=["onesb", f"H{4 + (i - 1) % 2}"], writes=pk(cb))

                    def g_(i):
                        cb = CB[i % 2]
                        P.op("act", lambda e: e.activation(out=Gt[i % 2][:, 0:512], in_=ps[cb][:], func=AF.Exp, scale=-1.0),
                             reads=pk(cb), writes=["F4"])

                    def w_(i):
                        P.op("dve", lambda e: e.tensor_tensor(out=Wb[i % 2][:], in0=E[i % 4][:, 0:512], in1=Gt[i % 2][:, 0:512], op=ALU.mult),
                             reads=[f"F{i % 4}", "F4"], writes=[f"H{2 + i % 2}"])

                    def ss_(i):
                        if i == 0:
                            P.op("pool", lambda e: e.tensor_copy(out=SS[0][:], in_=SP[0][:]), reads=["H0"], writes=["H4"])
                        else:
                            P.op("pool", lambda e: e.tensor_tensor(out=SS[i % 2][:], in0=SS[(i - 1) % 2][:], in1=SP[i % 2][:], op=ALU.add),
                                 reads=[f"H{4 + (i - 1) % 2}", f"H{i % 2}"], writes=[f"H{4 + i % 2}"])

                    def pv(i):
                        kind, v = units[i]
                        kb = 4 * g + v if kind == "d" else v
                        P.op("pe", lambda e: e.matmul(ps[3][:], lhsT=VV[:, h, kb, :], rhs=Wb[i % 2][:], start=(i == 0), stop=(i == n - 1)),
                             reads=[f"VV{kb}", f"H{2 + i % 2}"], writes=pk(3))

                    qk(0)
                    if n > 1:
                        qk(1)
                    e_(0)
                    for i in range(n + 2):
                        if i < n:
                            sp_(i)
                            cmm(i)
                            if i + 1 < n:
                                e_(i + 1)
                            if i + 2 < n:
                                qk(i + 2)
                        if 1 <= i <= n:
                            g_(i - 1)
                            w_(i - 1)
                        if i + 1 < n:
                            ss_(i)
                        if 2 <= i:
                            pv(i - 2)
                    P.op("dve", lambda e, h=h: e.scalar_tensor_tensor(out=yT[:, 4 + h, :], in0=ps[3][:], scalar=0.5, in1=gsb[:, h, :],
                                                                       op0=ALU.mult, op1=ALU.mult), reads=pk(3) + [f"gsb{h}"], writes=[f"yT.b{h}"])
                capB = P.capture_end()
                capA = []
                if g + 1 < NG:
                    P.capture_begin()
                    emit_A0(g + 1)
                    capA = P.capture_end()
                elif l + 1 < depth:
                    P.capture_begin()
                    save = list(ceng)
                    ceng[:] = ["dve"]
                    P.op("sp", lambda e: e.dma_start(out=gains[:], in_=gains_ds[l + 1]), writes=["gains"], dma=True)
                    load_w(wfm_ds[l + 1], Wfm, "Wfm", NFM, 0)
                    load_w(wtm_ds[l + 1], Wtm, "Wtm", NTM, 0)
                    ceng[:] = save
                    capA = P.capture_end()
                P.merge(capB, P.interleave(capA5, capA))
                P.capture_begin()
                yk = [f"yT.r{c}" for c in range(4)] + [f"yT.s{c}" for c in range(4)] + ["yT.m0", "yT.m1", "yT.b0", "yT.b1"]
                for c in range(4):
                    cs = slice(c * 128, (c + 1) * 128)
                    for half in range(2):
                        for fb in range(8):
                            P.op("pe", lambda e, fb=fb, half=half, cs=cs: e.matmul(ps[4 + half][:], lhsT=yT[:, fb, cs],
                                                                                    rhs=Wout[:, fb, half * 512:(half + 1) * 512],
                                                                                    start=(fb == 0), stop=(fb == 7)),
                                 reads=yk + ["Wout"], writes=pk(4 + half))
                    ot, otk = ((ox, "ox"), (xt, "xt"))[c % 2]
                    P.op("act", lambda e: e.activation(out=ot[:, 0:512], in_=ps[4][:], func=AF.Copy), reads=pk(4), writes=[otk])
                    P.op("dve", lambda e: e.tensor_copy(out=ot[:, 512:1024], in_=ps[5][:]), reads=pk(5), writes=[otk])
                    P.op("sp", lambda e, c=c: e.dma_start(out=Pd[l][t0 + c * 128:t0 + (c + 1) * 128, :], in_=ot[:]),
                         reads=[otk], writes=[f"P{l}.{g * 4 + c}"], dma=True)
                capC = P.capture_end()
                capA1 = []
                if g + 1 < NG:
                    P.capture_begin()
                    emit_A1(g + 1)
                    capA1 = P.capture_end()
                P.merge(capC, capA1)
                if g % 2 == 1 or g == NG - 1:
                    r0 = (g - (g % 2)) * 512
                    r1 = (g + 1) * 512
                    cks = list(range(r0 // 128, r1 // 128))
                    P.op("pool", lambda e: e.collective_compute("AllReduce", ALU.add, replica_groups=rgroups,
                                                                ins=[Pd[l][r0:r1, :]], outs=[Yd[l][r0:r1, :]]),
                         reads=[f"P{l}.{c}" for c in cks], writes=[f"Y{l}.{c}" for c in cks], dma="cc")
        lf = depth - 1
        P.op("act", lambda e: e.dma_start(out=rt[:], in_=postg_ds[lf].partition_broadcast(128)), writes=["rt"], dma=True)
        KTK = [f"KT{h}.{g}" for h in range(2) for g in range(NG)]
        VVK = [f"VV{cc}" for cc in range(NCH)]
        nx = max(1, min(3, (2 * S) // 2048))
        ktf = KT[:].rearrange("p h s -> p (h s)")
        vvf = VV[:].rearrange("p h c d -> p (h c d)")
        sets = [None] + [(ktf[:, i * 2048:(i + 1) * 2048].bitcast(F32), KTK, vvf[:, i * 2048:(i + 1) * 2048].bitcast(F32), VVK) for i in range(nx)]
        sets = [None] + [(sx[0], [f"fk{i}"], sx[2], [f"fv{i}"]) for i, sx in enumerate(sets[1:])]
        first_use = set()
        for ck in range(NCH):
            rsl = slice(ck * 128, (ck + 1) * 128)
            st = sets[ck % len(sets)]
            if st is None:
                combine_rows(lf, rsl, ck)
                P.op("sp", lambda e: e.dma_start(out=out_d[rsl, :], in_=xt[:]), reads=["xt"], writes=[f"out{ck}", "xt"], dma=True)
            else:
                if ck % len(sets) not in first_use:
                    first_use.add(ck % len(sets))
                    st = (st[0], st[1] + KTK, st[2], st[3] + VVK)
                combine_rows(lf, rsl, ck, bufs=st)
                P.op("sp", lambda e: e.dma_start(out=out_d[rsl, :], in_=st[0]), reads=st[1], writes=[f"out{ck}"] + st[1], dma=True)
        P.op("sp", lambda e: None, reads=[f"out{ck}" for ck in range(NCH)])
        P.emit(sems, dsems)
        stats = P.stats
    return nc, stats


def _consts():
    p = np.arange(128)[:, None]
    f = np.arange(128)[None, :]
    ident = (p == f)
    LE = (p <= f)
    GE = (p >= f)
    LT = (p < f)
    GT = (p > f)
    u = np.arange(896)[None, :] - 384
    neg = np.where(p < u, 0.0, -30000.0)
    ones = np.ones((128, 128))
    return np.concatenate([ident, LE, GE, LT, GT, neg, ones], axis=1).astype(np.float32)


def _rope_tables(S, hh):
    nch = S // 128
    d = 128
    inv_freq = (10000.0 ** (-np.arange(0, d, 2, dtype=np.float32) / np.float32(d))).astype(np.float32)
    pos = np.arange(S, dtype=np.float32)
    ang = (pos[:, None] * inv_freq[None, :]).astype(np.float32)
    cos = np.cos(ang).astype(np.float32).reshape(nch, 128, 64)
    sin = np.sin(ang).astype(np.float32).reshape(nch, 128, 64)
    cc = np.concatenate([cos, cos], -1)
    ss = np.concatenate([-sin, sin], -1)
    n = np.arange(128, dtype=np.float64)
    tab = np.zeros((nch, 128, 2, 4, 128), np.float32)
    for h in range(2):
        gam = 1.0 - 2.0 ** (-5.0 - (2 * hh + h))
        sq = gam ** (n + 1.0)
        sk = (128.0 ** -0.5) * gam ** (-(n + 1.0))
        tab[:, :, 0, h, :] = cc * sq[None, :, None]
        tab[:, :, 1, h, :] = ss * sq[None, :, None]
        tab[:, :, 0, 2 + h, :] = cc * sk[None, :, None]
        tab[:, :, 1, 2 + h, :] = ss * sk[None, :, None]
    return tab.reshape(nch, 128, 1024)


def _layer_inputs(l, hh, w_in, conv_w, conv_b, dt_bias, a_log, d_skip, ssd_norm_g, pre_norm_g, mem_norm_g, w_mem_kv, w_out):
    o = 256 * hh
    W = w_in[l]
    fm_cols = np.r_[3592 + o:3592 + o + 256, 4104 + o:4104 + o + 256, 5128 + o:5128 + o + 256, 5640 + o:5640 + o + 256,
                    6152 + o:6152 + o + 256, 2048 + o:2048 + o + 256, 2560 + 128 * hh:2560 + 128 * hh + 128,
                    2816 + 128 * hh:2816 + 128 * hh + 128]
    tm_cols = np.r_[0 + o:o + 256, 512 + o:512 + o + 256, 1024 + o:1024 + o + 256, 1536 + o:1536 + o + 256,
                    4616 + o:4616 + o + 256, 3072 + o:3072 + o + 256, 3584 + 4 * hh:3584 + 4 * hh + 4]
    ch = np.r_[o:o + 256, 512 + 128 * hh:512 + 128 * hh + 128, 768 + 128 * hh:768 + 128 * hh + 128]
    convp = np.zeros((128, 20), np.float32)
    cw = conv_w[l][:, ch]
    cbias = conv_b[l][ch]
    for cb in range(4):
        for i in range(4):
            convp[:, cb * 4 + i] = cw[i, cb * 128:(cb + 1) * 128]
        convp[:, 16 + cb] = cbias[cb * 128:(cb + 1) * 128]
    rowv = np.zeros((1, 528), np.float32)
    rowv[0, 0:4] = dt_bias[l][4 * hh:4 * hh + 4]
    rowv[0, 4:8] = a_log[l][4 * hh:4 * hh + 4]
    for h in range(2):
        rowv[0, 8 + h] = np.float32((1.0 - 2.0 ** (-5.0 - (2 * hh + h))) ** 128)
    rowv[0, 16:272] = np.repeat(d_skip[l][4 * hh:4 * hh + 4], 64)
    rowv[0, 272:528] = ssd_norm_g[l][o:o + 256]
    gains = np.zeros((128, 16), np.float32)
    gains[:, 0:8] = pre_norm_g[l].reshape(8, 128).T
    gains[:, 8:16] = mem_norm_g[l].reshape(8, 128).T
    wm = w_mem_kv[l]
    wmem = np.concatenate([wm[:, o:o + 256], wm[:, 512 + o:512 + o + 256]], axis=1)
    rows = np.r_[o:o + 256, 512 + o:512 + o + 256, 1024 + o:1024 + o + 256, 1536 + o:1536 + o + 256]
    return dict(wfm=np.ascontiguousarray(W[:, fm_cols]), wtm=np.ascontiguousarray(W[:, tm_cols]),
                wout=np.ascontiguousarray(w_out[l][rows, :]), wmem=np.ascontiguousarray(wmem),
                gains=gains, convp=convp, rowv=rowv)


def kernel(x, mem, pre_norm_g, post_norm_g, w_in, conv_w, conv_b, dt_bias, a_log, d_skip,
           ssd_norm_g, mem_norm_g, w_mem_kv, w_out):
    x = np.asarray(x, np.float32)
    mem = np.asarray(mem, np.float32)
    args = [np.asarray(a, np.float32) for a in (w_in, conv_w, conv_b, dt_bias, a_log, d_skip, ssd_norm_g, pre_norm_g, mem_norm_g, w_mem_kv, w_out)]
    post_norm_g = np.asarray(post_norm_g, np.float32)
    B, S, _ = x.shape
    depth = args[0].shape[0]
    assert B * 2 == 8
    in_maps = make_in_maps(x, mem, args, post_norm_g, list(range(8)))
    nc, _ = build_fused(S, depth)
    res = run_bass_kernel_spmd(nc, in_maps, core_ids=list(range(8)))
    return np.stack([res.results[2 * b]["out"] for b in range(B)], axis=0).astype(np.float32)


def make_in_maps(x, mem, args, post_norm_g, cores):
    S = x.shape[1]
    depth = args[0].shape[0]
    cf = _consts()
    rtabs = [_rope_tables(S, hh) for hh in range(2)]
    in_maps = []
    for core in cores:
        b, hh = core // 2, core % 2
        m = dict(x=np.ascontiguousarray(x[b]), memx=np.ascontiguousarray(mem[b]), cf=cf, rtab=rtabs[hh])
        for l in range(depth):
            for k, v in _layer_inputs(l, hh, *args).items():
                m[f"{k}{l}"] = v
            m[f"postg{l}"] = np.ascontiguousarray(post_norm_g[l].reshape(1, D))
        in_maps.append(m)
    return in_maps
```
